# Optimizing a Trainium2 kernel written in Bass

```python
import math
import jax, jax.numpy as jnp
from jax import lax
import numpy as np


D_MODEL = 1024
BATCH = 2
SEQ = 8192
DEPTH = 1

EPS = 1e-5
D_SSD = D_MODEL
SSD_HEAD_DIM = 64
SSD_HEADS = D_SSD // SSD_HEAD_DIM
SSD_GROUPS = 2
SSD_HEADS_PER_GROUP = SSD_HEADS // SSD_GROUPS
SSD_STATE = 128
CONV_WIDTH = 4
CHUNK = 128
D_XBC = D_SSD + 2 * SSD_GROUPS * SSD_STATE
D_ATTN = D_MODEL
ATTN_HEADS = 8
ATTN_QK_DIM = 64
ATTN_V_DIM = D_ATTN // ATTN_HEADS
D_QK = ATTN_HEADS * 2 * ATTN_QK_DIM
Q_BLOCK = 128
D_MIX = D_SSD + D_ATTN
SPLITS = [D_SSD,
          D_SSD + D_XBC,
          D_SSD + D_XBC + SSD_HEADS,
          D_SSD + D_XBC + SSD_HEADS + D_QK,
          D_SSD + D_XBC + SSD_HEADS + 2 * D_QK,
          D_SSD + D_XBC + SSD_HEADS + 2 * D_QK + D_ATTN]
D_IN_PROJ = SPLITS[-1] + D_ATTN

kernel_name = 'hybrid_ssd_diffattn_block'


def rms_norm(x, g):
    xf = x.astype(jnp.float32)
    y = xf * lax.rsqrt(jnp.mean(xf * xf, axis=-1, keepdims=True) + EPS)
    return (y * g.astype(jnp.float32)).astype(x.dtype)


def gated_group_rms_norm(y, z, g):
    bsz, seq, _ = y.shape
    t = (y.astype(jnp.float32) * jax.nn.silu(z.astype(jnp.float32)))
    t = t.reshape(bsz, seq, SSD_GROUPS, D_SSD // SSD_GROUPS)
    t = t * lax.rsqrt(jnp.mean(t * t, axis=-1, keepdims=True) + EPS)
    return (t.reshape(bsz, seq, D_SSD) * g.astype(jnp.float32)).astype(z.dtype)


def causal_depthwise_conv(u, w, b):
    ch = u.shape[-1]
    out = lax.conv_general_dilated(
        u, w[:, None, :], window_strides=(1,), padding=[(CONV_WIDTH - 1, 0)],
        dimension_numbers=('NWC', 'WIO', 'NWC'), feature_group_count=ch)
    return out + b


def ssd_chunked(x, dt, a, b_mat, c_mat):
    bsz, seq = x.shape[:2]
    nc = seq // CHUNK
    G, R, P, N = SSD_GROUPS, SSD_HEADS_PER_GROUP, SSD_HEAD_DIM, SSD_STATE
    xr = (x.astype(jnp.float32) * dt[..., None]).reshape(bsz, nc, CHUNK, G, R, P)
    adt = (dt * a).reshape(bsz, nc, CHUNK, G, R).transpose(0, 3, 4, 1, 2)
    a_cs = jnp.cumsum(adt, axis=-1)
    br = b_mat.astype(jnp.float32).reshape(bsz, nc, CHUNK, G, N)
    cr = c_mat.astype(jnp.float32).reshape(bsz, nc, CHUNK, G, N)
    causal = jnp.tril(jnp.ones((CHUNK, CHUNK), dtype=bool))
    seg = a_cs[..., :, None] - a_cs[..., None, :]
    lmat = jnp.exp(jnp.where(causal, seg, -jnp.inf))
    cb = jnp.einsum('bclgn,bcsgn->bgcls', cr, br)
    y_diag = jnp.einsum('bgcls,bgrcls,bcsgrp->bclgrp', cb, lmat, xr)
    decay = jnp.exp(a_cs[..., -1:] - a_cs)
    states = jnp.einsum('bclgn,bgrcl,bclgrp->bcgrpn', br, decay, xr)
    chunk_decay = jnp.exp(a_cs[..., -1])

    def step(hs, inp):
        s_c, d_c = inp
        return hs * d_c[..., None, None] + s_c, hs

    h0 = jnp.zeros((bsz, G, R, P, N), jnp.float32)
    _, prev = lax.scan(step, h0, (jnp.moveaxis(states, 1, 0), jnp.moveaxis(chunk_decay, -1, 0)))
    prev = jnp.moveaxis(prev, 0, 1)
    y_off = jnp.einsum('bclgn,bcgrpn,bgrcl->bclgrp', cr, prev, jnp.exp(a_cs))
    return (y_diag + y_off).reshape(bsz, seq, SSD_HEADS, P)


def alibi_slopes():
    return 2.0 ** (-(8.0 / ATTN_HEADS) * jnp.arange(1, ATTN_HEADS + 1, dtype=jnp.float32))


def diff_attention(q, k, v, lam, slopes):
    bsz, seq = q.shape[:2]
    nb = seq // Q_BLOCK
    qh = jnp.transpose(q, (0, 2, 3, 1, 4)) * (ATTN_QK_DIM ** -0.5)
    kh = jnp.transpose(k, (0, 2, 3, 1, 4))
    vh = jnp.transpose(v, (0, 2, 1, 3))
    qb = jnp.moveaxis(qh.reshape(bsz, ATTN_HEADS, 2, nb, Q_BLOCK, ATTN_QK_DIM), 3, 0)
    kpos = jnp.arange(seq)

    def block(args):
        qblk, i = args
        s = jnp.einsum('bhmqd,bhmkd->bhmqk', qblk, kh).astype(jnp.float32)
        qpos = i * Q_BLOCK + jnp.arange(Q_BLOCK)
        dist = qpos[:, None] - kpos[None, :]
        bias = -slopes[:, None, None, None] * dist.astype(jnp.float32)
        s = jnp.where(dist >= 0, s + bias, -jnp.inf)
        p = jax.nn.softmax(s, axis=-1)
        w = p[:, :, 0] - lam * p[:, :, 1]
        return jnp.einsum('bhqk,bhke->bhqe', w.astype(vh.dtype), vh)

    out = lax.map(block, (qb, jnp.arange(nb)))
    return jnp.transpose(out, (1, 0, 3, 2, 4)).reshape(bsz, seq, ATTN_HEADS, ATTN_V_DIM)


def setup_inputs(seed: int = 0) -> dict:
    key = jax.random.key(seed)
    ks = jax.random.split(key, 18)
    f32 = jnp.float32
    x = jax.random.normal(ks[0], (BATCH, SEQ, D_MODEL), f32)
    norm_gain = 1.0 + 0.02 * jax.random.normal(ks[1], (DEPTH, D_MODEL), f32)
    w_in = jax.random.normal(ks[2], (DEPTH, D_MODEL, D_IN_PROJ), f32) * D_MODEL ** -0.5
    conv_w = jax.random.normal(ks[3], (DEPTH, CONV_WIDTH, D_XBC), f32) * CONV_WIDTH ** -0.5
    conv_b = 0.02 * jax.random.normal(ks[4], (DEPTH, D_XBC), f32)
    u = jax.random.uniform(ks[5], (DEPTH, SSD_HEADS), f32)
    dt0 = jnp.exp(u * (math.log(0.1) - math.log(0.001)) + math.log(0.001))
    dt_bias = dt0 + jnp.log(-jnp.expm1(-dt0))
    a_log = jnp.log(jax.random.uniform(ks[6], (DEPTH, SSD_HEADS), f32, 1.0, 16.0))
    d_skip = 1.0 + 0.1 * jax.random.normal(ks[7], (DEPTH, SSD_HEADS), f32)
    ssd_norm_gain = 1.0 + 0.02 * jax.random.normal(ks[8], (DEPTH, D_SSD), f32)
    lambda_q1 = 0.1 * jax.random.normal(ks[9], (DEPTH, ATTN_QK_DIM), f32)
    lambda_k1 = 0.1 * jax.random.normal(ks[10], (DEPTH, ATTN_QK_DIM), f32)
    lambda_q2 = 0.1 * jax.random.normal(ks[11], (DEPTH, ATTN_QK_DIM), f32)
    lambda_k2 = 0.1 * jax.random.normal(ks[12], (DEPTH, ATTN_QK_DIM), f32)
    subln_gain = 1.0 + 0.02 * jax.random.normal(ks[13], (DEPTH, ATTN_V_DIM), f32)
    w_out = jax.random.normal(ks[14], (DEPTH, D_MIX, D_MODEL), f32) * D_MIX ** -0.5
    final_norm_gain = 1.0 + 0.02 * jax.random.normal(ks[15], (D_MODEL,), f32)
    return {'x': x, 'norm_gain': norm_gain, 'w_in': w_in, 'conv_w': conv_w,
            'conv_b': conv_b, 'dt_bias': dt_bias, 'a_log': a_log, 'd_skip': d_skip,
            'ssd_norm_gain': ssd_norm_gain, 'lambda_q1': lambda_q1,
            'lambda_k1': lambda_k1, 'lambda_q2': lambda_q2, 'lambda_k2': lambda_k2,
            'subln_gain': subln_gain, 'w_out': w_out, 'final_norm_gain': final_norm_gain}


def reference(x, norm_gain, w_in, conv_w, conv_b, dt_bias, a_log, d_skip,
              ssd_norm_gain, lambda_q1, lambda_k1, lambda_q2, lambda_k2,
              subln_gain, w_out, final_norm_gain):
    bsz, seq, _ = x.shape
    slopes = alibi_slopes()
    h = x
    for layer in range(DEPTH):
        u = rms_norm(h, norm_gain[layer])
        proj = jnp.einsum('bsd,de->bse', u, w_in[layer])
        z_ssd, xbc, dt_raw, q, k, v, z_attn = jnp.split(proj, SPLITS, axis=-1)

        xbc = jax.nn.silu(causal_depthwise_conv(xbc, conv_w[layer], conv_b[layer]))
        xs, bm, cm = jnp.split(xbc, [D_SSD, D_SSD + SSD_GROUPS * SSD_STATE], axis=-1)
        dt = jax.nn.softplus(dt_raw.astype(jnp.float32) + dt_bias[layer].astype(jnp.float32))
        a = -jnp.exp(a_log[layer].astype(jnp.float32))
        xs_h = xs.reshape(bsz, seq, SSD_HEADS, SSD_HEAD_DIM)
        y = ssd_chunked(xs_h, dt, a,
                        bm.reshape(bsz, seq, SSD_GROUPS, SSD_STATE),
                        cm.reshape(bsz, seq, SSD_GROUPS, SSD_STATE))
        y = y + xs_h.astype(jnp.float32) * d_skip[layer].astype(jnp.float32)[:, None]
        y_ssd = gated_group_rms_norm(y.reshape(bsz, seq, D_SSD), z_ssd, ssd_norm_gain[layer])

        lambda_init = 0.8 - 0.6 * math.exp(-0.3 * layer)
        lam = (jnp.exp(jnp.sum(lambda_q1[layer].astype(jnp.float32) * lambda_k1[layer].astype(jnp.float32)))
               - jnp.exp(jnp.sum(lambda_q2[layer].astype(jnp.float32) * lambda_k2[layer].astype(jnp.float32)))
               + lambda_init)
        attn = diff_attention(q.reshape(bsz, seq, ATTN_HEADS, 2, ATTN_QK_DIM),
                              k.reshape(bsz, seq, ATTN_HEADS, 2, ATTN_QK_DIM),
                              v.reshape(bsz, seq, ATTN_HEADS, ATTN_V_DIM), lam, slopes)
        attn = rms_norm(attn.astype(h.dtype), subln_gain[layer]) * (1.0 - lambda_init)
        y_attn = attn.reshape(bsz, seq, D_ATTN) * jax.nn.silu(z_attn)

        mixed = jnp.concatenate([y_ssd, y_attn.astype(h.dtype)], axis=-1)
        h = h + jnp.einsum('bse,ed->bsd', mixed, w_out[layer]).astype(h.dtype)
    return rms_norm(h, final_norm_gain)
```

```python
import math
from contextlib import ExitStack

import numpy as np
import concourse.bass as bass
import concourse.mybir as mybir
from concourse.bass_utils import run_bass_kernel_spmd

F32 = mybir.dt.float32
BF16 = mybir.dt.bfloat16
AF = mybir.ActivationFunctionType
ALU = mybir.AluOpType
AX = mybir.AxisListType

EPS = 1e-5
D = 1024
KC = 8
NCOL = 6672
C_Z, C_XS, C_B, C_C, C_DT, C_Q, C_K, C_V, C_ZA = 0, 1024, 2048, 2304, 2560, 2576, 3600, 4624, 5648
NEG = -30000.0
LAMBDA_INIT = 0.8 - 0.6 * math.exp(-0.3 * 0)
SLOPES = [2.0 ** (-(i + 1)) for i in range(8)]


class Prog:
    ENGS = ("pe", "act", "dve", "pool", "sp")

    def __init__(self):
        self.streams = {e: [] for e in self.ENGS}
        self.lastw = {}
        self.readers = {}
        self.dma_cnt = {}
        self.all_dma = []

    def add(self, eng, fn, reads=(), writes=(), dma=None):
        st = self.streams[eng]
        nid = (eng, len(st))
        raw, other = set(), set()
        for r in reads:
            if r in self.lastw:
                raw.add(self.lastw[r])
        for w in writes:
            if w in self.lastw:
                other.add(self.lastw[w])
            other.update(self.readers.get(w, ()))
        node = dict(fn=fn, raw=raw, other=other - raw, dma=None, sig=False, cnt=None, id=nid)
        if dma is not None:
            k = self.dma_cnt.get(dma, 0) + 1
            self.dma_cnt[dma] = k
            node["dma"] = (dma, 16 * k)
            self.all_dma.append(nid)
        st.append(node)
        for w in writes:
            self.lastw[w] = nid
            self.readers[w] = []
        for r in reads:
            self.readers.setdefault(r, []).append(nid)
        return nid

    def barrier(self):
        lasts = [(e, len(self.streams[e]) - 1) for e in self.ENGS if self.streams[e]]
        dmas = list(self.all_dma)
        for e in self.ENGS:
            nid = (e, len(self.streams[e]))
            deps = set(x for x in lasts if x[0] != e) | set(dmas)
            self.streams[e].append(dict(fn=None, raw=set(), other=deps, dma=None, sig=False, cnt=None, id=nid))
        self.lastw = {}
        self.readers = {}

    def wait_nodes(self, eng, nids):
        nid = (eng, len(self.streams[eng]))
        self.streams[eng].append(dict(fn=None, raw=set(), other=set(nids), dma=None, sig=False, cnt=None, id=nid))

    def node(self, nid):
        return self.streams[nid[0]][nid[1]]

    def _deps(self, node):
        eng = node["id"][0]
        out = []
        for d in node["raw"]:
            dn = self.node(d)
            if d[0] == eng and dn["dma"] is None and eng == "pe":
                continue
            out.append(d)
        for d in node["other"]:
            dn = self.node(d)
            if d[0] == eng and dn["dma"] is None:
                continue
            out.append(d)
        return out

    def emit(self, nc, es):
        for e in self.ENGS:
            for node in self.streams[e]:
                for d in self._deps(node):
                    dn = self.node(d)
                    if dn["dma"] is None:
                        dn["sig"] = True
        for e in self.ENGS:
            c = 0
            for node in self.streams[e]:
                if node["sig"]:
                    c += 1
                    node["cnt"] = c
        esem = {e: es.enter_context(nc.semaphore("s_" + e)) for e in self.ENGS}
        dsem = {k: es.enter_context(nc.semaphore("d_" + k)) for k in self.dma_cnt}
        block = es.enter_context(nc.Block())
        prog = self

        def run(e, engobj):
            known = {}
            for node in prog.streams[e]:
                need = {}
                for d in prog._deps(node):
                    dn = prog.node(d)
                    if dn["dma"] is not None:
                        key, val = ("d", dn["dma"][0]), dn["dma"][1]
                    else:
                        key, val = ("e", d[0]), dn["cnt"]
                    if val > need.get(key, 0):
                        need[key] = val
                for key, val in need.items():
                    if known.get(key, 0) >= val:
                        continue
                    sem = dsem[key[1]] if key[0] == "d" else esem[key[1]]
                    engobj.wait_ge(sem, val)
                    known[key] = val
                if node["fn"] is None:
                    continue
                ins = node["fn"](engobj)
                if node["dma"] is not None:
                    ins.then_inc(dsem[node["dma"][0]], 16)
                elif node["sig"]:
                    ins.then_inc(esem[e], 1)

        @block.tensor
        def _(eng):
            run("pe", eng)

        @block.scalar
        def _(eng):
            run("act", eng)

        @block.vector
        def _(eng):
            run("dve", eng)

        @block.gpsimd
        def _(eng):
            run("pool", eng)

        @block.sync
        def _(eng):
            run("sp", eng)


def cst_layout(S):
    NT = S // 128
    NSB = S // 1024
    off = {}
    c = 0
    for name, n in (("gainT", 8), ("convw", 48), ("convb", 12), ("dtb", 16), ("alog", 16), ("dskip", 16),
                    ("lamv", 256), ("subln", 1), ("ident", 128), ("tri", 128), ("ustr", 128),
                    ("maskA", 256), ("maskB", 256), ("kpos", NT), ("valid", NT), ("qref", NSB),
                    ("ssdg", 1024), ("fing", 1024), ("ones", 128), ("gainbc", 1024)):
        off[name] = (c, n)
        c += n
    return off, c


import os


def build(S):
    STAGE = int(os.environ.get('MK_STAGE', '99'))
    NT = S // 128
    NB = S // 512
    NSB = S // 1024
    OWN = NSB * 256
    NOT = OWN // 128
    off, NCST = cst_layout(S)

    nc = bass.Bass("TRN2", target_bir_lowering=False)
    x_d = nc.dram_tensor("x", [S, D], F32, kind="ExternalInput").ap()
    win_d = nc.dram_tensor("w_in", [D, NCOL], F32, kind="ExternalInput").ap()
    wout_d = nc.dram_tensor("w_out", [2048, D], F32, kind="ExternalInput").ap()
    cst_d = nc.dram_tensor("cst", [128, NCST], F32, kind="ExternalInput").ap()
    out_d = nc.dram_tensor("out", [OWN, D], F32, kind="ExternalOutput").ap()
    kT_d = nc.dram_tensor("kT_scr", [8, 128, S], BF16, kind="Internal").ap()
    v_d = nc.dram_tensor("v_scr", [8, 128, NT * 128], BF16, kind="Internal").ap()
    mt_d = nc.dram_tensor("mt_scr", [OWN, D], BF16, kind="Internal").ap()
    ma_d = nc.dram_tensor("ma_scr", [8, 128, OWN], BF16, kind="Internal").ap()

    p = Prog()
    es = ExitStack()

    def sb(name, shape, dt):
        return es.enter_context(nc.sbuf_tensor(name, shape, dt))

    def ps(name):
        return es.enter_context(nc.psum_tensor(name, [128, 512], F32))

    CST = sb("CST", [128, NCST], F32)
    WA = sb("WA", [128, 16384], BF16)
    WB = sb("WB", [128, 16384], BF16)
    WC = sb("WC", [128, 8, 528], BF16)
    G = [sb("G%d" % i, [128, 8192], BF16) for i in range(4)]
    XS = sb("XS", [128, 2, 1024], F32)
    FS = sb("FS", [128, 7176], F32)
    SM = sb("SM", [128, 512], F32)
    BM = sb("BM", [128, 2048], BF16)
    CB16 = sb("CB16", [128, 768], BF16)
    PTS = sb("PTS", [128, 3, 512], BF16)
    PB_ = [ps("PS%d" % i) for i in range(8)]
    PA, PBk, PT, PM, PS0, PS1, PY0, PY1 = PB_

    def cs_(name, lo=0, hi=None):
        o, n = off[name]
        hi = n if hi is None else hi
        return CST[:, o + lo:o + hi]

    Wk = WA[:, 0:8192].rearrange("p (k c) -> p k c", k=8)
    Wv = WA[:, 8192:16384].rearrange("p (k c) -> p k c", k=8)
    Wx = WB[:, 0:8192].rearrange("p (k c) -> p k c", k=8)
    Wz = WB[:, 8192:16384].rearrange("p (k c) -> p k c", k=8)
    Wq, Wza = Wk, Wv
    uTown = WB[:, 0:8 * OWN].rearrange("p (k t) -> p k t", k=8)
    Wo = WB[:, :].rearrange("p (k c) -> p k c", k=16)
    uT = [G[0][:, 0:4096].rearrange("p (k t) -> p k t", k=8), G[0][:, 4096:8192].rearrange("p (k t) -> p k t", k=8)]
    kst = G[1][:, 0:4096].rearrange("p (h t) -> p h t", h=8)
    vst = G[1][:, 4096:8192].rearrange("p (h t d) -> p h t d", h=8, t=4)
    xsT = G[2][:, 0:5120].rearrange("p (c t) -> p c t", c=10)
    xn = G[2][:, 5120:7168].rearrange("p (s c) -> p s c", s=2)
    Hb = G[2][:, 7168:8192]
    xs_tm = G[3][:, 0:4096].rearrange("p (t c) -> p t c", t=4)
    B_tm = G[3][:, 4096:5120].rearrange("p (t c) -> p t c", t=4)
    xrd = G[3][:, 5120:7168].rearrange("p (s c) -> p s c", s=2)
    xr = G[3][:, 7168:8192]
    raw = FS[:, 0:1030].rearrange("p (s t) -> p s t", s=2)
    acc = FS[:, 1030:2054].rearrange("p (s t) -> p s t", s=2)
    H = FS[:, 2054:3078]
    zs = FS[:, 3078:4102]
    A_ = FS[:, 4102:4614].rearrange("p (h t) -> p h t", h=4)
    Lt = FS[:, 4614:5126].rearrange("p (h t) -> p h t", h=4)
    yt0 = FS[:, 5126:6150]
    yt1 = FS[:, 6150:7174]
    halo = SM[:, 0:30].rearrange("p (c t) -> p c t", c=10)
    MT = BM[:, 0:512].rearrange("p (h t) -> p h t", h=4)
    mixed = BM[:, 512:1536]
    Cact = BM[:, 1536:2048].rearrange("p (g t) -> p g t", g=2)
    Craw = yt1[:, 0:520].rearrange("p (g t) -> p g t", g=2)
    ident_bf = CB16[:, 0:128]
    ones_bf = CB16[:, 128:256]
    maskA_bf = CB16[:, 256:512]
    maskB_bf = CB16[:, 512:768]
    ssq = SM[:, 32:36]
    rstd = SM[:, 36:40]
    dtz = SM[:, 64:128].rearrange("p (t h) -> p t h", t=4)
    dta = SM[:, 128:192].rearrange("p (t h) -> p t h", t=4)
    dte = SM[:, 192:256].rearrange("p (t h) -> p t h", t=4)
    dtv = SM[:, 256:320].rearrange("p (t h) -> p t h", t=4)
    adt = SM[:, 320:384].rearrange("p (t h) -> p t h", t=4)
    a_bc = SM[:, 384:400]
    csb = SM[:, 400:416]
    tot = SM[:, 416:432]
    dec = SM[:, 432:448]
    cd = SM[:, 448:464]
    wsc = SM[:, 464:480]
    ecs = SM[:, 480:496]
    lam4 = SM[:, 496:500]
    nlam = SM[:, 500:501]
    gsc = SM[:, 501:502]
    gss = SM[:, 502:504]
    grs = SM[:, 504:506]

    st = dict(pab=0, xslot=0, xnslot=0, acc=0, pts=0)

    def next_pab():
        st["pab"] ^= 1
        return (PA, "PA") if st["pab"] else (PBk, "PB")

    p.add("sp", lambda e: e.dma_start(out=CST[:], in_=cst_d[:, :]), writes=["CST"], dma="cst")

    def load_w(dst_view, col0, ncols, key, semname):
        src = win_d[:, col0:col0 + ncols].rearrange("(k p) c -> p k c", p=128)
        p.add("pool", lambda e: e.dma_start(out=dst_view, in_=src), writes=[key], dma=semname)

    load_w(Wk, C_K, 1024, "WA0", "wA0")
    load_w(Wx, C_XS, 1024, "WB0", "wB0")
    load_w(WC[:, :, 0:256], C_B, 256, "WC", "wC")
    load_w(WC[:, :, 256:512], C_C, 256, "WC", "wC")
    load_w(WC[:, :, 512:528], C_DT, 16, "WC", "wC")
    load_w(Wv, C_V, 1024, "WA1", "wA1")
    load_w(Wz, C_Z, 1024, "WB1", "wB1")

    p.add("dve", lambda e: e.tensor_copy(out=CB16[:, 0:128], in_=cs_("ident")), reads=["CST"], writes=["CB16"])
    p.add("dve", lambda e: e.memset(CB16[:, 128:256], 1.0), writes=["CB16"])
    p.add("dve", lambda e: e.tensor_copy(out=CB16[:, 256:512], in_=cs_("maskA")), reads=["CST"], writes=["CB16"])
    p.add("dve", lambda e: e.tensor_copy(out=CB16[:, 512:768], in_=cs_("maskB")), reads=["CST"], writes=["CB16"])
    p.add("dve", lambda e: e.memset(H, 0.0), writes=["H"])
    p.add("dve", lambda e: e.memset(SM[:, 0:30], 0.0), writes=[("halo", c_) for c_ in range(10)])
    p.add("act", lambda e: e.activation(out=a_bc, in_=cs_("alog"), func=AF.Exp), reads=["CST"], writes=["a_bc"])
    p.add("dve", lambda e: e.tensor_scalar(out=a_bc, in0=a_bc, scalar1=-1.0, scalar2=None, op0=ALU.mult),
          reads=["a_bc"], writes=["a_bc"])
    lv = cs_("lamv")
    p.add("dve", lambda e: e.tensor_tensor(out=yt0[:, 0:64], in0=lv[:, 0:64], in1=lv[:, 64:128], op=ALU.mult),
          reads=["CST"], writes=["yt0"])
    p.add("dve", lambda e: e.tensor_tensor(out=yt0[:, 64:128], in0=lv[:, 128:192], in1=lv[:, 192:256], op=ALU.mult),
          reads=["CST"], writes=["yt0"])
    p.add("dve", lambda e: e.tensor_reduce(out=lam4[:, 0:2], in_=yt0[:, 0:128].rearrange("p (a b) -> p a b", a=2),
                                           axis=AX.X, op=ALU.add), reads=["yt0"], writes=["lam4"])
    p.add("act", lambda e: e.activation(out=lam4[:, 2:4], in_=lam4[:, 0:2], func=AF.Exp), reads=["lam4"], writes=["lam4b"])
    p.add("dve", lambda e: e.tensor_tensor(out=nlam, in0=lam4[:, 3:4], in1=lam4[:, 2:3], op=ALU.subtract),
          reads=["lam4b"], writes=["nlam"])
    p.add("dve", lambda e: e.tensor_scalar(out=nlam, in0=nlam, scalar1=-LAMBDA_INIT, scalar2=None, op0=ALU.add),
          reads=["nlam"], writes=["nlam"])
    p.add("dve", lambda e: e.tensor_scalar(out=gsc, in0=cs_("subln"), scalar1=1.0 - LAMBDA_INIT, scalar2=None, op0=ALU.mult),
          reads=["CST"], writes=["gsc"])

    def emit_rstd(dst, src, n, rkey, wkey, tmpkey):
        p.add("dve", lambda e: e.tensor_scalar(out=dst, in0=src, scalar1=1.0 / n, scalar2=EPS, op0=ALU.mult, op1=ALU.add),
              reads=[rkey], writes=[tmpkey])
        p.add("act", lambda e: e.activation(out=dst, in_=dst, func=AF.Ln), reads=[tmpkey], writes=[tmpkey + "l"])
        p.add("act", lambda e: e.activation(out=dst, in_=dst, func=AF.Exp, scale=-0.5), reads=[tmpkey + "l"], writes=[wkey])

    def emit_xT(row0, dst, dst_key):
        s = st["xslot"]
        st["xslot"] = (s + 1) % 2
        xk = "XS%d" % s
        xt = XS[:, s, :]
        p.add("sp", lambda e: e.dma_start(out=xt, in_=x_d[row0:row0 + 128, :]), writes=[xk], dma="x%d" % s)
        n = st["xnslot"]
        st["xnslot"] ^= 1
        xnk = "xn%d" % n
        sq, rs = ssq[:, n:n + 1], rstd[:, n:n + 1]
        p.add("act", lambda e: e.activation(out=xn[:, n, :], in_=xt, func=AF.Square, accum_out=sq),
              reads=[xk], writes=[xnk, "ssq%d" % n])
        XL = int(os.environ.get('MK_X', '9'))
        if XL < 1:
            return
        emit_rstd(rs, sq, float(D), "ssq%d" % n, "rstd%d" % n, "rstdt%d" % n)
        if XL < 2:
            return
        gbc = cs_("gainbc")
        p.add("dve", lambda e: e.scalar_tensor_tensor(out=xn[:, n, :], in0=xt, scalar=rs, in1=gbc, op0=ALU.mult, op1=ALU.mult),
              reads=[xk, "rstd%d" % n, "CST"], writes=[xnk])
        if XL < 3:
            return
        ptb = PT[:, 0:512].bitcast(BF16).rearrange("p (k t) -> p k t", k=8)
        for kc in range(KC):
            p.add("pe", lambda e, kc=kc: e.transpose(out=ptb[:, kc, :], in_=xn[:, n, kc * 128:(kc + 1) * 128], identity=ident_bf),
                  reads=[xnk, "CB16"], writes=["PT"])
        if XL < 4:
            return
        gT = cs_("gainT")
        XV = int(os.environ.get('MK_XV', '1'))
        if XV == 1:
            p.add("dve", lambda e: e.tensor_copy(out=dst, in_=ptb), reads=["PT", "CST"], writes=[dst_key])
        elif XV == 2:
            for kc in range(KC):
                p.add("dve", lambda e, kc=kc: e.tensor_scalar(out=dst[:, kc, :], in0=ptb[:, kc, :], scalar1=gT[:, kc:kc + 1], scalar2=None, op0=ALU.mult),
                      reads=["PT", "CST"], writes=[dst_key])
        else:
            p.add("dve", lambda e: e.tensor_tensor(out=dst, in0=ptb, in1=gT.unsqueeze(2).to_broadcast([128, 8, 128]), op=ALU.mult),
                  reads=["PT", "CST"], writes=[dst_key])

    def emit_conv(src_raw, n_out, chunk, dst, dst_key, raw_key):
        a = st["acc"]
        st["acc"] ^= 1
        ak = "acc%d" % a
        ac = acc[:, a, 0:n_out]
        cw = cs_("convw")
        cb = cs_("convb")
        w = [cw[:, chunk * 4 + k:chunk * 4 + k + 1] for k in range(4)]
        p.add("dve", lambda e: e.tensor_scalar(out=ac, in0=src_raw[:, 0:n_out], scalar1=w[0], scalar2=None, op0=ALU.mult),
              reads=[raw_key, "CST"], writes=[ak])
        for k in (1, 2, 3):
            p.add("dve", lambda e, k=k: e.scalar_tensor_tensor(out=ac, in0=src_raw[:, k:k + n_out], scalar=w[k], in1=ac,
                                                               op0=ALU.mult, op1=ALU.add),
                  reads=[raw_key, ak, "CST"], writes=[ak])
        p.add("act", lambda e: e.activation(out=dst, in_=ac, func=AF.Silu, bias=cb[:, chunk:chunk + 1]),
              reads=[ak, "CST"], writes=[dst_key])

    own_row = [0]

    def do_block(blk):
        us = blk % 2
        uk = "uT%d" % us
        u = uT[us]
        if STAGE < 1:
            return
        for t in range(4):
            emit_xT(blk * 512 + t * 128, u[:, :, t * 128:(t + 1) * 128], uk)

        if STAGE < 2:
            return
        for h in range(8):
            pa, pk = next_pab()
            for kc in range(KC):
                p.add("pe", lambda e, h=h, kc=kc, pa=pa: e.matmul(pa[:, :], Wk[:, kc, h * 128:(h + 1) * 128], u[:, kc, :],
                                                              start=(kc == 0), stop=(kc == KC - 1)),
                      reads=[uk, "WA0"], writes=[pk])
            p.add("act", lambda e, h=h, pa=pa: e.activation(out=kst[:, h, :], in_=pa[:, :], func=AF.Copy),
                  reads=[pk], writes=["kst"])
        p.add("sp", lambda e, blk=blk: e.dma_start(out=kT_d[:, :, blk * 512:(blk + 1) * 512].rearrange("h p t -> p h t"), in_=kst),
              reads=["kst"], writes=[("kTd", blk)], dma="kst")

        if STAGE < 3:
            return
        for t in range(4):
            for hf in range(2):
                pa, pk = next_pab()
                for kc in range(KC):
                    p.add("pe", lambda e, t=t, hf=hf, kc=kc, pa=pa: e.matmul(pa[:, :], u[:, kc, t * 128:(t + 1) * 128],
                                                                            Wv[:, kc, hf * 512:(hf + 1) * 512],
                                                                            start=(kc == 0), stop=(kc == KC - 1)),
                          reads=[uk, "WA1"], writes=[pk])
                p.add("dve", lambda e, t=t, hf=hf, pa=pa: e.tensor_copy(out=vst[:, hf * 4:(hf + 1) * 4, t, :],
                                                                       in_=pa[:, :].rearrange("p (h d) -> p h d", h=4)),
                      reads=[pk], writes=["vst"])
        p.add("sp", lambda e, blk=blk: e.dma_start(
            out=v_d[:, :, blk * 512:(blk + 1) * 512].rearrange("h p (t d) -> p h t d", t=4), in_=vst),
            reads=["vst"], writes=[("vd", blk)], dma="vst")

        if STAGE < 4:
            return
        for ch in range(10):
            pa, pk = next_pab()
            wsrc = Wx[:, :, ch * 128:(ch + 1) * 128] if ch < 8 else WC[:, :, (ch - 8) * 128:(ch - 7) * 128]
            wkey = "WB0" if ch < 8 else "WC"
            for kc in range(KC):
                p.add("pe", lambda e, kc=kc, pa=pa, wsrc=wsrc: e.matmul(pa[:, :], wsrc[:, kc, :], u[:, kc, :],
                                                                    start=(kc == 0), stop=(kc == KC - 1)),
                      reads=[uk, wkey], writes=[pk])
            rs_ = ch % 2
            rk = "raw%d" % rs_
            p.add("pool", lambda e, ch=ch, rs_=rs_: e.tensor_copy(out=raw[:, rs_, 0:3], in_=halo[:, ch, :]),
                  reads=[("halo", ch)], writes=[rk])
            p.add("act", lambda e, pa=pa, rs_=rs_: e.activation(out=raw[:, rs_, 3:515], in_=pa[:, :], func=AF.Copy),
                  reads=[pk], writes=[rk])
            p.add("pool", lambda e, ch=ch, rs_=rs_: e.tensor_copy(out=halo[:, ch, :], in_=raw[:, rs_, 512:515]),
                  reads=[rk], writes=[("halo", ch)])
            emit_conv(raw[:, rs_, :], 512, ch, xsT[:, ch, :], "xsT", rk)

        if STAGE < 5:
            return
        ptb = PT[:, 0:512].bitcast(BF16).rearrange("p (k t) -> p k t", k=8)
        for t in range(4):
            for ch in range(8):
                p.add("pe", lambda e, t=t, ch=ch: e.transpose(out=ptb[:, ch, :], in_=xsT[:, ch, t * 128:(t + 1) * 128], identity=ident_bf),
                      reads=["xsT", "CB16"], writes=["PT"])
            p.add("dve", lambda e, t=t: e.tensor_copy(out=xs_tm[:, t, :].rearrange("p (k c) -> p k c", k=8), in_=ptb),
                  reads=["PT"], writes=[("xs_tm", t)])
            for g in range(2):
                p.add("pe", lambda e, t=t, g=g: e.transpose(out=ptb[:, g, :], in_=xsT[:, 8 + g, t * 128:(t + 1) * 128], identity=ident_bf),
                      reads=["xsT", "CB16"], writes=["PT"])
            p.add("dve", lambda e, t=t: e.tensor_copy(out=B_tm[:, t, :].rearrange("p (k c) -> p k c", k=2), in_=ptb[:, 0:2, :]),
                  reads=["PT"], writes=[("B_tm", t)])

        if STAGE < 6:
            return
        pmd = PM[:, 256:320].rearrange("p (t h) -> p t h", t=4)
        for t in range(4):
            for kc in range(KC):
                p.add("pe", lambda e, t=t, kc=kc: e.matmul(pmd[:, t, :], u[:, kc, t * 128:(t + 1) * 128], WC[:, kc, 512:528],
                                                           start=(kc == 0), stop=(kc == KC - 1)),
                      reads=[uk, "WC"], writes=["PM"])
        dtb = cs_("dtb")
        p.add("dve", lambda e: e.tensor_tensor(out=dtz, in0=pmd, in1=dtb.unsqueeze(1).to_broadcast([128, 4, 16]), op=ALU.add),
              reads=["PM", "CST"], writes=["dtz"])
        p.add("act", lambda e: e.activation(out=dta, in_=dtz, func=AF.Abs), reads=["dtz"], writes=["dta"])
        p.add("act", lambda e: e.activation(out=dte, in_=dta, func=AF.Exp, scale=-1.0), reads=["dta"], writes=["dte"])
        p.add("act", lambda e: e.activation(out=dte, in_=dte, func=AF.Ln, bias=1.0), reads=["dte"], writes=["dtl"])
        p.add("dve", lambda e: e.scalar_tensor_tensor(out=dtv, in0=dtz, scalar=0.0, in1=dte, op0=ALU.max, op1=ALU.add),
              reads=["dtz", "dtl"], writes=["dtv"])
        vl = cs_("valid")
        p.add("dve", lambda e, blk=blk: e.tensor_tensor(out=dtv, in0=dtv,
                                                        in1=vl[:, blk * 4:(blk + 1) * 4].unsqueeze(2).to_broadcast([128, 4, 16]), op=ALU.mult),
              reads=["dtv", "CST"], writes=["dtv"])
        p.add("dve", lambda e: e.tensor_tensor(out=adt, in0=dtv, in1=a_bc.unsqueeze(1).to_broadcast([128, 4, 16]), op=ALU.mult),
              reads=["dtv", "a_bc"], writes=["adt"])

        own_blk = (blk % 2 == 1) and STAGE >= 8
        if STAGE < 7:
            return
        if own_blk:
            for g in range(2):
                pa, pk = next_pab()
                for kc in range(KC):
                    p.add("pe", lambda e, g=g, kc=kc, pa=pa: e.matmul(pa[:, 0:260], WC[:, kc, 256 + g * 128:256 + (g + 1) * 128],
                                                                  u[:, kc, 252:512], start=(kc == 0), stop=(kc == KC - 1)),
                          reads=[uk, "WC"], writes=[pk])
                p.add("act", lambda e, g=g, pa=pa: e.activation(out=Craw[:, g, :], in_=pa[:, 0:260], func=AF.Copy),
                      reads=[pk], writes=["yt1"])
                emit_conv(Craw[:, g, 1:260], 256, 10 + g, Cact[:, g, :], "Cact", "yt1")

        for t in range(4):
            own = own_blk and t >= 2
            tri = cs_("tri")
            p.add("pe", lambda e, t=t: e.matmul(PM[:, 320:336], tri, adt[:, t, :], start=True, stop=True),
                  reads=["adt", "CST"], writes=["PM"])
            p.add("pe", lambda e, t=t: e.matmul(PM[:, 336:352], cs_("ones"), adt[:, t, :], start=True, stop=True),
                  reads=["adt", "CST"], writes=["PM"])
            p.add("dve", lambda e: e.tensor_copy(out=csb, in_=PM[:, 320:336]), reads=["PM"], writes=["csb"])
            p.add("dve", lambda e: e.tensor_copy(out=tot, in_=PM[:, 336:352]), reads=["PM"], writes=["tot"])
            p.add("dve", lambda e: e.tensor_tensor(out=dec, in0=tot, in1=csb, op=ALU.subtract), reads=["tot", "csb"], writes=["dec"])
            p.add("act", lambda e: e.activation(out=dec, in_=dec, func=AF.Exp), reads=["dec"], writes=["dece"])
            p.add("act", lambda e: e.activation(out=cd, in_=tot, func=AF.Exp), reads=["tot"], writes=["cd"])
            p.add("dve", lambda e, t=t: e.tensor_tensor(out=wsc, in0=dec, in1=dtv[:, t, :], op=ALU.mult),
                  reads=["dece", "dtv"], writes=["wsc"])
            xs_ = t % 2
            xk_ = "xrd%d" % xs_
            p.add("pool", lambda e, t=t, xs_=xs_: e.tensor_tensor(out=xrd[:, xs_, :].rearrange("p (h d) -> p h d", h=16),
                                                                 in0=xs_tm[:, t, :].rearrange("p (h d) -> p h d", h=16),
                                                                 in1=wsc.unsqueeze(2).to_broadcast([128, 16, 64]), op=ALU.mult),
                  reads=[("xs_tm", t), "wsc"], writes=[xk_])
            for g in range(2):
                psg = PS0 if g == 0 else PS1
                p.add("pe", lambda e, t=t, g=g, psg=psg, xs_=xs_: e.matmul(psg[:, :], B_tm[:, t, g * 128:(g + 1) * 128],
                                                                         xrd[:, xs_, g * 512:(g + 1) * 512], start=True, stop=True),
                      reads=[("B_tm", t), xk_], writes=["PS%d" % g])
            if own:
                p.add("pool", lambda e: e.tensor_copy(out=Hb, in_=H), reads=["H"], writes=["Hb"])
            p.add("dve", lambda e: e.tensor_tensor(out=H.rearrange("p (h d) -> p h d", h=16), in0=H.rearrange("p (h d) -> p h d", h=16),
                                                   in1=cd.unsqueeze(2).to_broadcast([128, 16, 64]), op=ALU.mult),
                  reads=["H", "cd"], writes=["H"])
            for g in range(2):
                psg = PS0 if g == 0 else PS1
                p.add("dve", lambda e, g=g, psg=psg: e.tensor_tensor(out=H[:, g * 512:(g + 1) * 512], in0=H[:, g * 512:(g + 1) * 512],
                                                                   in1=psg[:, :], op=ALU.add),
                      reads=["H", "PS%d" % g], writes=["H"])
            if not own:
                continue

            tt = t - 2
            for hf in range(2):
                pa, pk = next_pab()
                for kc in range(KC):
                    p.add("pe", lambda e, t=t, hf=hf, kc=kc, pa=pa: e.matmul(pa[:, :], u[:, kc, t * 128:(t + 1) * 128],
                                                                            Wz[:, kc, hf * 512:(hf + 1) * 512],
                                                                            start=(kc == 0), stop=(kc == KC - 1)),
                          reads=[uk, "WB1"], writes=[pk])
                p.add("act", lambda e, hf=hf, pa=pa: e.activation(out=zs[:, hf * 512:(hf + 1) * 512], in_=pa[:, :], func=AF.Silu),
                      reads=[pk], writes=["zs"])
            CBm = yt1[:, 768:1024].rearrange("p (g t) -> p g t", g=2)
            for g in range(2):
                p.add("pe", lambda e, t=t, g=g, tt=tt: e.matmul(PM[:, g * 128:(g + 1) * 128], xsT[:, 8 + g, t * 128:(t + 1) * 128],
                                                              Cact[:, g, tt * 128:(tt + 1) * 128], start=True, stop=True),
                      reads=["xsT", "Cact"], writes=["PM"])
                p.add("dve", lambda e, g=g: e.tensor_tensor(out=CBm[:, g, :], in0=PM[:, g * 128:(g + 1) * 128], in1=tri, op=ALU.mult),
                      reads=["PM", "CST"], writes=["CBm"])
            p.add("pool", lambda e, t=t: e.tensor_tensor(out=xr.rearrange("p (h d) -> p h d", h=16),
                                                        in0=xs_tm[:, t, :].rearrange("p (h d) -> p h d", h=16),
                                                        in1=dtv[:, t, :].unsqueeze(2).to_broadcast([128, 16, 64]), op=ALU.mult),
                  reads=[("xs_tm", t), "dtv"], writes=["xr"])
            ustr = cs_("ustr")
            for hq in range(4):
                g = hq // 2
                pa, pk = next_pab()
                for hh in range(4):
                    h = hq * 4 + hh
                    p.add("dve", lambda e, t=t, h=h, hh=hh: e.tensor_scalar(out=A_[:, hh, :], in0=ustr, scalar1=adt[:, t, h:h + 1],
                                                                            scalar2=None, op0=ALU.mult),
                          reads=["adt", "CST"], writes=[("A", hh)])
                    p.add("pe", lambda e, hh=hh, pa=pa: e.matmul(pa[:, hh * 128:(hh + 1) * 128], A_[:, hh, :], tri, start=True, stop=True),
                          reads=[("A", hh), "CST"], writes=[pk])
                p.add("act", lambda e, pa=pa: e.activation(out=Lt.rearrange("p h t -> p (h t)"), in_=pa[:, :], func=AF.Exp),
                      reads=[pk], writes=["Lt"])
                p.add("dve", lambda e, g=g: e.tensor_tensor(out=MT, in0=Lt, in1=CBm[:, g, :].unsqueeze(1).to_broadcast([128, 4, 128]), op=ALU.mult),
                      reads=["Lt", "CBm"], writes=["MT"])
                for hh in range(4):
                    h = hq * 4 + hh
                    py = PY0 if h < 8 else PY1
                    c0 = (h % 8) * 64
                    p.add("pe", lambda e, hh=hh, h=h, py=py, c0=c0: e.matmul(py[:, c0:c0 + 64], MT[:, hh, :], xr[:, h * 64:(h + 1) * 64],
                                                                          start=True, stop=True),
                          reads=["MT", "xr"], writes=["PY%d" % (h // 8)])
            for g in range(2):
                psg = PS0 if g == 0 else PS1
                p.add("pe", lambda e, g=g, psg=psg, tt=tt: e.matmul(psg[:, :], Cact[:, g, tt * 128:(tt + 1) * 128], Hb[:, g * 512:(g + 1) * 512],
                                                                  start=True, stop=True),
                      reads=["Cact", "Hb"], writes=["PS%d" % g])
            p.add("act", lambda e: e.activation(out=ecs, in_=csb, func=AF.Exp), reads=["csb"], writes=["ecs"])
            dsk = cs_("dskip")
            for g in range(2):
                psg = PS0 if g == 0 else PS1
                py = PY0 if g == 0 else PY1
                ysl = yt0[:, g * 512:(g + 1) * 512]
                p.add("act", lambda e, psg=psg, ysl=ysl: e.activation(out=ysl, in_=psg[:, :], func=AF.Copy),
                      reads=["PS%d" % g], writes=["yt0"])
                p.add("dve", lambda e, g=g, ysl=ysl: e.tensor_tensor(
                    out=ysl.rearrange("p (h d) -> p h d", h=8), in0=ysl.rearrange("p (h d) -> p h d", h=8),
                    in1=ecs[:, g * 8:(g + 1) * 8].unsqueeze(2).to_broadcast([128, 8, 64]), op=ALU.mult),
                    reads=["yt0", "ecs"], writes=["yt0"])
                p.add("dve", lambda e, py=py, ysl=ysl: e.tensor_tensor(out=ysl, in0=ysl, in1=py[:, :], op=ALU.add),
                      reads=["yt0", "PY%d" % g], writes=["yt0"])
            ysk = yt1[:, 0:512]
            for g in range(2):
                ysl = yt0[:, g * 512:(g + 1) * 512]
                p.add("dve", lambda e, t=t, g=g: e.tensor_tensor(
                    out=ysk.rearrange("p (h d) -> p h d", h=8), in0=xs_tm[:, t, g * 512:(g + 1) * 512].rearrange("p (h d) -> p h d", h=8),
                    in1=dsk[:, g * 8:(g + 1) * 8].unsqueeze(2).to_broadcast([128, 8, 64]), op=ALU.mult),
                    reads=[("xs_tm", t), "CST"], writes=["yt1"])
                p.add("dve", lambda e, ysl=ysl: e.tensor_tensor(out=ysl, in0=ysl, in1=ysk, op=ALU.add), reads=["yt0", "yt1"], writes=["yt0"])
                p.add("dve", lambda e, g=g, ysl=ysl: e.tensor_tensor(out=ysl, in0=ysl, in1=zs[:, g * 512:(g + 1) * 512], op=ALU.mult),
                      reads=["yt0", "zs"], writes=["yt0"])
                p.add("act", lambda e, g=g, ysl=ysl: e.activation(out=ysk, in_=ysl, func=AF.Square, accum_out=gss[:, g:g + 1]),
                      reads=["yt0"], writes=["yt1", "gss"])
            emit_rstd(grs, gss, 512.0, "gss", "grs", "grst")
            sg = cs_("ssdg")
            for g in range(2):
                ysl = yt0[:, g * 512:(g + 1) * 512]
                p.add("dve", lambda e, g=g, ysl=ysl: e.scalar_tensor_tensor(out=mixed[:, g * 512:(g + 1) * 512], in0=ysl, scalar=grs[:, g:g + 1],
                                                                          in1=sg[:, g * 512:(g + 1) * 512], op0=ALU.mult, op1=ALU.mult),
                      reads=["yt0", "grs", "CST"], writes=["mixed"])
            r0 = own_row[0]
            own_row[0] += 128
            p.add("sp", lambda e, r0=r0: e.dma_start(out=mt_d[r0:r0 + 128, :], in_=mixed), reads=["mixed"], writes=[("mtd", r0)], dma="mix")

    for blk in range(NB):
        do_block(blk)
    p.barrier()
    if STAGE < 9:
        p.emit(nc, es)
        es.close()
        return nc

    load_w(Wq, C_Q, 1024, "WA0", "wA0")
    load_w(Wza, C_ZA, 1024, "WA1", "wA1")
    for ot in range(NOT):
        i, tt = divmod(ot, 2)
        emit_xT(i * 1024 + 768 + tt * 128, uTown[:, :, ot * 128:(ot + 1) * 128], "uTown")

    kpos = cs_("kpos")
    qref = cs_("qref")
    QT = FS[:, 5126:5126 + OWN].bitcast(BF16).rearrange("p (m t) -> p m t", m=2)
    p.add("pool", lambda e: e.memset(QT, 0.0), writes=["QT"])
    zsT = FS[:, 0:OWN]
    rL = FS[:, 2054:2566]
    On = FS[:, 2566:3078]
    Aa = FS[:, 3078:3334]
    Asq = FS[:, 3334:3590]
    rs2 = FS[:, 3590:3846]
    biasT = [FS[:, 4102 + s_ * 512:4102 + s_ * 512 + NSB * NT].rearrange("p (i n) -> p i n", i=NSB) for s_ in range(2)]
    st["sslot"] = 0
    st["ya"] = 0
    ones32 = cs_("ones")

    for h in range(8):
        ks = h % 2
        kTh = G[ks][:, 0:S]
        Vh = G[2 + ks][:, 0:NT * 128].rearrange("p (t d) -> p t d", d=128)
        kk, vk, bk = "kT%d" % ks, "V%d" % ks, "biasT%d" % ks
        p.add("sp", lambda e, h=h, kTh=kTh: e.dma_start(out=kTh, in_=kT_d[h, :, :]), writes=[kk], dma="kld%d" % ks)
        p.add("sp", lambda e, h=h, ks=ks: e.dma_start(out=G[2 + ks][:, 0:NT * 128], in_=v_d[h, :, :]), writes=[vk] + (["xn0", "xn1"] if ks == 0 else []), dma="vld%d" % ks)
        for i in range(NSB):
            p.add("dve", lambda e, i=i, h=h, ks=ks: e.tensor_scalar(out=biasT[ks][:, i, :], in0=kpos, scalar1=qref[:, i:i + 1],
                                                                   scalar2=SLOPES[h], op0=ALU.subtract, op1=ALU.mult),
                  reads=["CST"], writes=[bk])
        for c in range(0, OWN, 512):
            n_ = min(512, OWN - c)
            pa, pk = next_pab()
            for kc in range(KC):
                p.add("pe", lambda e, h=h, kc=kc, pa=pa, c=c, n_=n_: e.matmul(pa[:, 0:n_], Wq[:, kc, h * 128:(h + 1) * 128], uTown[:, kc, c:c + n_],
                                                                            start=(kc == 0), stop=(kc == KC - 1)),
                      reads=["uTown", "WA0"], writes=[pk])
            p.add("act", lambda e, pa=pa, c=c, n_=n_: e.activation(out=QT[0:64, 0, c:c + n_], in_=pa[0:64, 0:n_], func=AF.Copy), reads=[pk], writes=["QT"])
            p.add("act", lambda e, pa=pa, c=c, n_=n_: e.activation(out=QT[64:128, 1, c:c + n_], in_=pa[64:128, 0:n_], func=AF.Copy), reads=[pk], writes=["QT"])
            pa, pk = next_pab()
            for kc in range(KC):
                p.add("pe", lambda e, h=h, kc=kc, pa=pa, c=c, n_=n_: e.matmul(pa[:, 0:n_], Wza[:, kc, h * 128:(h + 1) * 128], uTown[:, kc, c:c + n_],
                                                                            start=(kc == 0), stop=(kc == KC - 1)),
                      reads=["uTown", "WA1"], writes=[pk])
            p.add("act", lambda e, pa=pa, c=c, n_=n_: e.activation(out=zsT[:, c:c + n_], in_=pa[:, 0:n_], func=AF.Silu), reads=[pk], writes=["zsT"])

        P2 = int(os.environ.get('MK_P2', '9'))
        for i in range(NSB):
            if P2 < 2:
                break
            nk = 8 * i + 8
            q0 = i * 256

            def emit_qk(n, i=i, nk=nk, q0=q0, kTh=kTh):
                ss = st["sslot"]
                st["sslot"] ^= 1
                Sb = PS0 if ss == 0 else PS1
                sk = "S%d" % ss
                diag = n >= nk - 2
                for m in range(2):
                    p.add("pe", lambda e, m=m, n=n, Sb=Sb, diag=diag: e.matmul(Sb[:, m * 256:(m + 1) * 256],
                                                                             kTh[:, n * 128:(n + 1) * 128],
                                                                             QT[:, m, q0:q0 + 256], start=True, stop=(not (diag and P2 >= 3))),
                          reads=[kk, "QT"], writes=[sk])
                    if diag and P2 >= 3:
                        mk = maskA_bf if n == nk - 2 else maskB_bf
                        p.add("pe", lambda e, m=m, Sb=Sb, mk=mk: e.matmul(Sb[:, m * 256:(m + 1) * 256], ident_bf, mk, start=False, stop=True),
                              reads=["CB16"], writes=[sk])
                return Sb, sk

            pend = emit_qk(0)
            for n in range(nk):
                Sb, sk = pend
                if n + 1 < nk:
                    pend = emit_qk(n + 1)
                pslot = st["pts"]
                st["pts"] = (pslot + 1) % 3
                pk_ = "PTS%d" % pslot
                p.add("act", lambda e, Sb=Sb, pslot=pslot, i=i, n=n, ks=ks: e.activation(out=PTS[:, pslot, :], in_=Sb[:, :], func=AF.Exp,
                                                                                      bias=biasT[ks][:, i, n:n + 1], scale=0.125),
                      reads=[sk, bk], writes=[pk_])
                if P2 < 3:
                    continue
                p.add("pe", lambda e, n=n, pslot=pslot, nk=nk, Vh=Vh: e.matmul(PY0[:, :], Vh[:, n, :], PTS[:, pslot, :], start=(n == 0), stop=(n == nk - 1)),
                      reads=[vk, pk_], writes=["PO"])
                p.add("pe", lambda e, n=n, pslot=pslot, nk=nk: e.matmul(PY1[:, :], ones_bf, PTS[:, pslot, :], start=(n == 0), stop=(n == nk - 1)),
                      reads=["CB16", pk_], writes=["PL"])

            if P2 < 4:
                continue
            p.add("dve", lambda e: e.reciprocal(out=rL, in_=PY1[:, :]), reads=["PL"], writes=["rL"])
            p.add("dve", lambda e: e.tensor_tensor(out=On, in0=PY0[:, :], in1=rL, op=ALU.mult), reads=["PO", "rL"], writes=["On"])
            p.add("dve", lambda e: e.scalar_tensor_tensor(out=Aa, in0=On[:, 256:512], scalar=nlam[:, 0:1], in1=On[:, 0:256], op0=ALU.mult, op1=ALU.add),
                  reads=["On", "nlam"], writes=["Aa"])
            p.add("act", lambda e: e.activation(out=Asq, in_=Aa, func=AF.Square), reads=["Aa"], writes=["Asq"])
            p.add("pe", lambda e: e.matmul(PM[:, 0:256], ones32, Asq, start=True, stop=True), reads=["Asq", "CST"], writes=["PM"])
            emit_rstd(rs2, PM[:, 0:256], 128.0, "PM", "rs2", "rs2t")
            p.add("dve", lambda e: e.tensor_tensor(out=Aa, in0=Aa, in1=rs2, op=ALU.mult), reads=["Aa", "rs2"], writes=["Aa"])
            ys_ = st["ya"]
            st["ya"] ^= 1
            yab = BM[:, ys_ * 256:(ys_ + 1) * 256]
            p.add("dve", lambda e, yab=yab, q0=q0: e.scalar_tensor_tensor(out=yab, in0=Aa, scalar=gsc[:, 0:1], in1=zsT[:, q0:q0 + 256], op0=ALU.mult, op1=ALU.mult),
                  reads=["Aa", "gsc", "zsT"], writes=["ya%d" % ys_])
            p.add("sp", lambda e, yab=yab, h=h, q0=q0: e.dma_start(out=ma_d[h, :, q0:q0 + 256], in_=yab), reads=["ya%d" % ys_],
                  writes=[("mad", h, i)], dma="ya%d" % ys_)

    p.barrier()
    if STAGE < 10:
        p.emit(nc, es)
        es.close()
        return nc
    p.add("pool", lambda e: e.dma_start(out=Wo, in_=wout_d.rearrange("(k p) c -> p k c", p=128)), writes=["Wo"], dma="wB0")
    hb = FS[:, 0:1024]
    fss = SM[:, 40:41]
    frs = SM[:, 41:42]
    fg = cs_("fing")
    outs = []
    ptb = PT[:, 0:512].bitcast(BF16).rearrange("p (k t) -> p k t", k=8)
    for ot in range(NOT):
        i, tt = divmod(ot, 2)
        row0 = i * 1024 + 768 + tt * 128
        ms = ot % 2
        mx = BM[:, ms * 1024:(ms + 1) * 1024]
        mTa = G[1][:, ms * 1024:(ms + 1) * 1024].rearrange("p (h t) -> p h t", h=8)
        mTs = G[0][:, ms * 1024:(ms + 1) * 1024].rearrange("p (h t) -> p h t", h=8)
        ob = FS[:, 1024 + ms * 1024:2048 + ms * 1024]
        p.add("sp", lambda e, mx=mx, ot=ot: e.dma_start(out=mx, in_=mt_d[ot * 128:(ot + 1) * 128, :]), writes=["mx%d" % ms], dma="mx%d" % ms)
        p.add("sp", lambda e, mTa=mTa, ot=ot: e.dma_start(out=mTa, in_=ma_d[:, :, ot * 128:(ot + 1) * 128].rearrange("h p t -> p h t")),
              writes=["mTa%d" % ms], dma="mta%d" % ms)
        s_ = st["xslot"]
        st["xslot"] = (s_ + 1) % 2
        xt = XS[:, s_, :]
        p.add("sp", lambda e, xt=xt, row0=row0: e.dma_start(out=xt, in_=x_d[row0:row0 + 128, :]), writes=["XS%d" % s_], dma="x%d" % s_)
        for kc in range(8):
            p.add("pe", lambda e, kc=kc, mx=mx: e.transpose(out=ptb[:, kc, :], in_=mx[:, kc * 128:(kc + 1) * 128], identity=ident_bf),
                  reads=["mx%d" % ms, "CB16"], writes=["PT"])
        p.add("dve", lambda e, mTs=mTs: e.tensor_copy(out=mTs, in_=ptb), reads=["PT"], writes=["mTs%d" % ms])
        for hf in range(2):
            pa, pk = next_pab()
            for kc in range(16):
                lh = mTs[:, kc, :] if kc < 8 else mTa[:, kc - 8, :]
                lk = ("mTs%d" % ms) if kc < 8 else ("mTa%d" % ms)
                p.add("pe", lambda e, kc=kc, hf=hf, pa=pa, lh=lh: e.matmul(pa[:, :], lh, Wo[:, kc, hf * 512:(hf + 1) * 512],
                                                                       start=(kc == 0), stop=(kc == 15)),
                      reads=[lk, "Wo"], writes=[pk])
            p.add("dve", lambda e, hf=hf, pa=pa, xt=xt: e.tensor_tensor(out=hb[:, hf * 512:(hf + 1) * 512], in0=pa[:, :],
                                                                      in1=xt[:, hf * 512:(hf + 1) * 512], op=ALU.add),
                  reads=[pk, "XS%d" % s_], writes=["hb"])
        p.add("act", lambda e, ob=ob: e.activation(out=ob, in_=hb, func=AF.Square, accum_out=fss), reads=["hb"], writes=["ob%d" % ms, "fss"])
        emit_rstd(frs, fss, float(D), "fss", "frs", "frst")
        p.add("dve", lambda e, ob=ob: e.scalar_tensor_tensor(out=ob, in0=hb, scalar=frs[:, 0:1], in1=fg, op0=ALU.mult, op1=ALU.mult),
              reads=["hb", "frs", "CST"], writes=["ob%d" % ms])
        outs.append(p.add("sp", lambda e, ob=ob, ot=ot: e.dma_start(out=out_d[ot * 128:(ot + 1) * 128, :], in_=ob), reads=["ob%d" % ms],
                          writes=[("outd", ot)], dma="ob%d" % ms))
    p.wait_nodes("sp", outs)
    p.emit(nc, es)
    es.close()
    return nc


_CACHE = {}


def _cst_table(S, j, params):
    NT = S // 128
    NSB = S // 1024
    off, ncst = cst_layout(S)
    t = np.zeros((128, ncst), np.float32)

    def put(name, arr):
        o, n = off[name]
        t[:, o:o + n] = arr

    pidx = np.arange(128)
    put("gainT", params["norm_gain"].reshape(8, 128).T)
    cw = params["conv_w"].reshape(4, 12, 128)
    put("convw", cw.transpose(2, 1, 0).reshape(128, 48))
    put("convb", params["conv_b"].reshape(12, 128).T)
    put("dtb", np.broadcast_to(params["dt_bias"].reshape(1, 16), (128, 16)))
    put("alog", np.broadcast_to(params["a_log"].reshape(1, 16), (128, 16)))
    put("dskip", np.broadcast_to(params["d_skip"].reshape(1, 16), (128, 16)))
    lam = np.concatenate([params["lambda_q1"].reshape(-1), params["lambda_k1"].reshape(-1),
                          params["lambda_q2"].reshape(-1), params["lambda_k2"].reshape(-1)])
    put("lamv", np.broadcast_to(lam.reshape(1, 256), (128, 256)))
    put("subln", params["subln_gain"].reshape(128, 1))
    put("ident", np.eye(128, dtype=np.float32))
    tri = (pidx[:, None] <= pidx[None, :]).astype(np.float32)
    put("tri", tri)
    put("ustr", 1.0 - tri)
    q = np.arange(256)
    put("maskA", np.where(pidx[:, None] <= q[None, :], 0.0, NEG).astype(np.float32))
    put("maskB", np.where(128 + pidx[:, None] <= q[None, :], 0.0, NEG).astype(np.float32))
    pad = (3 - j) * 256
    gpos = (np.arange(NT)[None, :] * 128 + pidx[:, None] - pad).astype(np.float32)
    put("kpos", np.where(gpos >= 0, gpos, -1.0e9).astype(np.float32))
    put("valid", (gpos >= 0).astype(np.float32))
    qr = (np.arange(NSB) * 1024 + 256 * j + 128).astype(np.float32)
    put("qref", np.broadcast_to(qr.reshape(1, NSB), (128, NSB)))
    put("ssdg", np.broadcast_to(params["ssd_norm_gain"].reshape(1, 1024), (128, 1024)))
    put("fing", np.broadcast_to(params["final_norm_gain"].reshape(1, 1024), (128, 1024)))
    put("ones", np.ones((128, 128), np.float32))
    put("gainbc", np.broadcast_to(params["norm_gain"].reshape(1, 1024), (128, 1024)))
    return t


def kernel(x, norm_gain, w_in, conv_w, conv_b, dt_bias, a_log, d_skip, ssd_norm_gain,
           lambda_q1, lambda_k1, lambda_q2, lambda_k2, subln_gain, w_out, final_norm_gain, _trace=False):
    x = np.asarray(x, np.float32)
    B, S, _ = x.shape
    assert B == 2 and S % 1024 == 0
    params = dict(norm_gain=np.asarray(norm_gain, np.float32), conv_w=np.asarray(conv_w, np.float32),
                  conv_b=np.asarray(conv_b, np.float32), dt_bias=np.asarray(dt_bias, np.float32),
                  a_log=np.asarray(a_log, np.float32), d_skip=np.asarray(d_skip, np.float32),
                  ssd_norm_gain=np.asarray(ssd_norm_gain, np.float32), lambda_q1=np.asarray(lambda_q1, np.float32),
                  lambda_k1=np.asarray(lambda_k1, np.float32), lambda_q2=np.asarray(lambda_q2, np.float32),
                  lambda_k2=np.asarray(lambda_k2, np.float32), subln_gain=np.asarray(subln_gain, np.float32),
                  final_norm_gain=np.asarray(final_norm_gain, np.float32))
    w_in2 = np.ascontiguousarray(np.asarray(w_in, np.float32).reshape(D, NCOL))
    w_out2 = np.ascontiguousarray(np.asarray(w_out, np.float32).reshape(2048, D))
    if S not in _CACHE:
        _CACHE[S] = build(S)
    nc = _CACHE[S]
    in_maps = []
    for c in range(8):
        b, j = divmod(c, 4)
        pad = (3 - j) * 256
        xc = np.zeros((S, D), np.float32)
        xc[pad:] = x[b, :S - pad]
        in_maps.append({"x": xc, "w_in": w_in2, "w_out": w_out2, "cst": _cst_table(S, j, params)})
    res = run_bass_kernel_spmd(nc, in_maps, core_ids=list(range(8)), trace=_trace)
    out = np.zeros((B, S, D), np.float32)
    NSB = S // 1024
    for c in range(8):
        b, j = divmod(c, 4)
        o = np.asarray(res.results[c]["out"], np.float32).reshape(NSB, 256, D)
        for i in range(NSB):
            g0 = i * 1024 + 256 * j
            out[b, g0:g0 + 256] = o[i]
    if _trace:
        kernel.last_res = res
    return out
```

```python
import math
from contextlib import ExitStack

import numpy as np
import concourse.bass as bass
import concourse.mybir as mybir
from concourse.bass_utils import run_bass_kernel_spmd

F32 = mybir.dt.float32
BF16 = mybir.dt.bfloat16
AF = mybir.ActivationFunctionType
ALU = mybir.AluOpType
AX = mybir.AxisListType

EPS = 1e-5
D = 1024
KC = 8
NCOL = 6672
C_Z, C_XS, C_B, C_C, C_DT, C_Q, C_K, C_V, C_ZA = 0, 1024, 2048, 2304, 2560, 2576, 3600, 4624, 5648
NEG = -30000.0
LAMBDA_INIT = 0.8 - 0.6 * math.exp(-0.3 * 0)
SLOPES = [2.0 ** (-(i + 1)) for i in range(8)]


class Prog:
    ENGS = ("pe", "act", "dve", "pool", "sp")

    def __init__(self):
        self.streams = {e: [] for e in self.ENGS}
        self.lastw = {}
        self.readers = {}
        self.dma_cnt = {}
        self.all_dma = []

    def add(self, eng, fn, reads=(), writes=(), dma=None):
        st = self.streams[eng]
        nid = (eng, len(st))
        raw, other = set(), set()
        for r in reads:
            if r in self.lastw:
                raw.add(self.lastw[r])
        for w in writes:
            if w in self.lastw:
                other.add(self.lastw[w])
            other.update(self.readers.get(w, ()))
        node = dict(fn=fn, raw=raw, other=other - raw, dma=None, sig=False, cnt=None, id=nid)
        if dma is not None:
            k = self.dma_cnt.get(dma, 0) + 1
            self.dma_cnt[dma] = k
            node["dma"] = (dma, 16 * k)
            self.all_dma.append(nid)
        st.append(node)
        for w in writes:
            self.lastw[w] = nid
            self.readers[w] = []
        for r in reads:
            self.readers.setdefault(r, []).append(nid)
        return nid

    def barrier(self):
        lasts = [(e, len(self.streams[e]) - 1) for e in self.ENGS if self.streams[e]]
        dmas = list(self.all_dma)
        for e in self.ENGS:
            nid = (e, len(self.streams[e]))
            deps = set(x for x in lasts if x[0] != e) | set(dmas)
            self.streams[e].append(dict(fn=None, raw=set(), other=deps, dma=None, sig=False, cnt=None, id=nid))
        self.lastw = {}
        self.readers = {}

    def wait_nodes(self, eng, nids):
        nid = (eng, len(self.streams[eng]))
        self.streams[eng].append(dict(fn=None, raw=set(), other=set(nids), dma=None, sig=False, cnt=None, id=nid))

    def node(self, nid):
        return self.streams[nid[0]][nid[1]]

    def _deps(self, node):
        eng = node["id"][0]
        out = []
        for d in node["raw"]:
            dn = self.node(d)
            if d[0] == eng and dn["dma"] is None and eng == "pe":
                continue
            out.append(d)
        for d in node["other"]:
            dn = self.node(d)
            if d[0] == eng and dn["dma"] is None:
                continue
            out.append(d)
        return out

    def emit(self, nc, es):
        for e in self.ENGS:
            for node in self.streams[e]:
                for d in self._deps(node):
                    dn = self.node(d)
                    if dn["dma"] is None:
                        dn["sig"] = True
        for e in self.ENGS:
            c = 0
            for node in self.streams[e]:
                if node["sig"]:
                    c += 1
                    node["cnt"] = c
        esem = {e: es.enter_context(nc.semaphore("s_" + e)) for e in self.ENGS}
        dsem = {k: es.enter_context(nc.semaphore("d_" + k)) for k in self.dma_cnt}
        block = es.enter_context(nc.Block())
        prog = self

        def run(e, engobj):
            known = {}
            for node in prog.streams[e]:
                need = {}
                for d in prog._deps(node):
                    dn = prog.node(d)
                    if dn["dma"] is not None:
                        key, val = ("d", dn["dma"][0]), dn["dma"][1]
                    else:
                        key, val = ("e", d[0]), dn["cnt"]
                    if val > need.get(key, 0):
                        need[key] = val
                for key, val in need.items():
                    if known.get(key, 0) >= val:
                        continue
                    sem = dsem[key[1]] if key[0] == "d" else esem[key[1]]
                    engobj.wait_ge(sem, val)
                    known[key] = val
                if node["fn"] is None:
                    continue
                ins = node["fn"](engobj)
                if node["dma"] is not None:
                    ins.then_inc(dsem[node["dma"][0]], 16)
                elif node["sig"]:
                    ins.then_inc(esem[e], 1)

        @block.tensor
        def _(eng):
            run("pe", eng)

        @block.scalar
        def _(eng):
            run("act", eng)

        @block.vector
        def _(eng):
            run("dve", eng)

        @block.gpsimd
        def _(eng):
            run("pool", eng)

        @block.sync
        def _(eng):
            run("sp", eng)


def cst_layout(S):
    NT = S // 128
    NSB = S // 1024
    off = {}
    c = 0
    for name, n in (("gainT", 8), ("convw", 48), ("convb", 12), ("dtb", 16), ("alog", 16), ("dskip", 16),
                    ("lamv", 256), ("subln", 1), ("ident", 128), ("tri", 128), ("ustr", 128),
                    ("maskA", 256), ("maskB", 256), ("kpos", NT), ("valid", NT), ("qref", NSB),
                    ("ssdg", 1024), ("fing", 1024), ("ones", 128), ("gainbc", 1024)):
        off[name] = (c, n)
        c += n
    return off, c


import os


def build(S):
    STAGE = int(os.environ.get('MK_STAGE', '99'))
    NT = S // 128
    NB = S // 512
    NSB = S // 1024
    OWN = NSB * 256
    NOT = OWN // 128
    off, NCST = cst_layout(S)

    nc = bass.Bass("TRN2", target_bir_lowering=False)
    x_d = nc.dram_tensor("x", [S, D], F32, kind="ExternalInput").ap()
    win_d = nc.dram_tensor("w_in", [D, NCOL], F32, kind="ExternalInput").ap()
    wout_d = nc.dram_tensor("w_out", [2048, D], F32, kind="ExternalInput").ap()
    cst_d = nc.dram_tensor("cst", [128, NCST], F32, kind="ExternalInput").ap()
    out_d = nc.dram_tensor("out", [OWN, D], F32, kind="ExternalOutput").ap()
    kT_d = nc.dram_tensor("kT_scr", [8, 128, S], BF16, kind="Internal").ap()
    v_d = nc.dram_tensor("v_scr", [8, 128, NT * 128], BF16, kind="Internal").ap()
    mt_d = nc.dram_tensor("mt_scr", [OWN, D], BF16, kind="Internal").ap()
    ma_d = nc.dram_tensor("ma_scr", [8, 128, OWN], BF16, kind="Internal").ap()

    p = Prog()
    es = ExitStack()

    def sb(name, shape, dt):
        return es.enter_context(nc.sbuf_tensor(name, shape, dt))

    def ps(name):
        return es.enter_context(nc.psum_tensor(name, [128, 512], F32))

    CST = sb("CST", [128, NCST], F32)
    WA = sb("WA", [128, 16384], BF16)
    WB = sb("WB", [128, 16384], BF16)
    WC = sb("WC", [128, 8, 528], BF16)
    G = [sb("G%d" % i, [128, 8192], BF16) for i in range(4)]
    XS = sb("XS", [128, 2, 1024], F32)
    FS = sb("FS", [128, 7176], F32)
    SM = sb("SM", [128, 1024], F32)
    BM = sb("BM", [128, 2048], BF16)
    CB16 = sb("CB16", [128, 768], BF16)
    PTS = sb("PTS", [128, 3, 512], BF16)
    PB_ = [ps("PS%d" % i) for i in range(8)]
    PA, PBk, PT, PM, PS0, PS1, PY0, PY1 = PB_

    def cs_(name, lo=0, hi=None):
        o, n = off[name]
        hi = n if hi is None else hi
        return CST[:, o + lo:o + hi]

    Wk = WA[:, 0:8192].rearrange("p (k c) -> p k c", k=8)
    Wv = WA[:, 8192:16384].rearrange("p (k c) -> p k c", k=8)
    Wx = WB[:, 0:8192].rearrange("p (k c) -> p k c", k=8)
    Wz = WB[:, 8192:16384].rearrange("p (k c) -> p k c", k=8)
    Wq, Wza = Wk, Wv
    uTown = WB[:, 0:8 * OWN].rearrange("p (k t) -> p k t", k=8)
    Wo = WB[:, :].rearrange("p (k c) -> p k c", k=16)
    uT = [G[0][:, 0:4096].rearrange("p (k t) -> p k t", k=8), G[0][:, 4096:8192].rearrange("p (k t) -> p k t", k=8)]
    kst = G[1][:, 0:4096].rearrange("p (h t) -> p h t", h=8)
    vst = G[1][:, 4096:8192].rearrange("p (h t d) -> p h t d", h=8, t=4)
    xsT = G[2][:, 0:5120].rearrange("p (c t) -> p c t", c=10)
    xn = G[2][:, 5120:7168].rearrange("p (s c) -> p s c", s=2)
    Hb = G[2][:, 7168:8192]
    xs_tm = G[3][:, 0:4096].rearrange("p (t c) -> p t c", t=4)
    B_tm = G[3][:, 4096:5120].rearrange("p (t c) -> p t c", t=4)
    xrd = G[3][:, 5120:7168].rearrange("p (s c) -> p s c", s=2)
    xr = G[3][:, 7168:8192]
    raw = FS[:, 0:1030].rearrange("p (s t) -> p s t", s=2)
    acc = FS[:, 1030:2054].rearrange("p (s t) -> p s t", s=2)
    H = FS[:, 2054:3078]
    zs = FS[:, 3078:4102]
    A_ = FS[:, 4102:4614].rearrange("p (h t) -> p h t", h=4)
    Lt = FS[:, 4614:5126].rearrange("p (h t) -> p h t", h=4)
    yt0 = FS[:, 5126:6150]
    yt1 = FS[:, 6150:7174]
    halo = SM[:, 0:30].rearrange("p (c t) -> p c t", c=10)
    MT = BM[:, 0:512].rearrange("p (h t) -> p h t", h=4)
    mixed = BM[:, 512:1536]
    Cact = BM[:, 1536:2048].rearrange("p (g t) -> p g t", g=2)
    Craw = yt1[:, 0:520].rearrange("p (g t) -> p g t", g=2)
    ident_bf = CB16[:, 0:128]
    ones_bf = CB16[:, 128:256]
    maskA_bf = CB16[:, 256:512]
    maskB_bf = CB16[:, 512:768]
    ssq = SM[:, 32:36]
    rstd = SM[:, 36:40]
    dtz = SM[:, 64:128].rearrange("p (t h) -> p t h", t=4)
    dta = SM[:, 128:192].rearrange("p (t h) -> p t h", t=4)
    dte = SM[:, 192:256].rearrange("p (t h) -> p t h", t=4)
    dtv = SM[:, 256:320].rearrange("p (t h) -> p t h", t=4)
    adt = SM[:, 320:384].rearrange("p (t h) -> p t h", t=4)
    a_bc = SM[:, 384:400]
    csb4 = SM[:, 512:576].rearrange("p (t h) -> p t h", t=4)
    tot4 = SM[:, 576:640].rearrange("p (t h) -> p t h", t=4)
    dec4 = SM[:, 640:704].rearrange("p (t h) -> p t h", t=4)
    cd4 = SM[:, 704:768].rearrange("p (t h) -> p t h", t=4)
    wsc4 = SM[:, 768:832].rearrange("p (t h) -> p t h", t=4)
    ecs = SM[:, 480:496]
    lam4 = SM[:, 496:500]
    nlam = SM[:, 500:501]
    gsc = SM[:, 501:502]
    gss = SM[:, 502:504]
    grs = SM[:, 504:506]

    st = dict(pab=0, xslot=0, xnslot=0, acc=0, pts=0)

    def next_pab():
        st["pab"] ^= 1
        return (PA, "PA") if st["pab"] else (PBk, "PB")

    p.add("sp", lambda e: e.dma_start(out=CST[:], in_=cst_d[:, :]), writes=["CST"], dma="cst")

    def load_w(dst_view, col0, ncols, key, semname):
        src = win_d[:, col0:col0 + ncols].rearrange("(k p) c -> p k c", p=128)
        p.add("pool", lambda e: e.dma_start(out=dst_view, in_=src), writes=[key], dma=semname)

    load_w(Wk, C_K, 1024, "WA0", "wA0")
    load_w(Wx, C_XS, 1024, "WB0", "wB0")
    load_w(WC[:, :, 0:256], C_B, 256, "WC", "wC")
    load_w(WC[:, :, 256:512], C_C, 256, "WC", "wC")
    load_w(WC[:, :, 512:528], C_DT, 16, "WC", "wC")
    load_w(Wv, C_V, 1024, "WA1", "wA1")
    load_w(Wz, C_Z, 1024, "WB1", "wB1")

    p.add("dve", lambda e: e.tensor_copy(out=CB16[:, 0:128], in_=cs_("ident")), reads=["CST"], writes=["CB16"])
    p.add("dve", lambda e: e.memset(CB16[:, 128:256], 1.0), writes=["CB16"])
    p.add("dve", lambda e: e.tensor_copy(out=CB16[:, 256:512], in_=cs_("maskA")), reads=["CST"], writes=["CB16"])
    p.add("dve", lambda e: e.tensor_copy(out=CB16[:, 512:768], in_=cs_("maskB")), reads=["CST"], writes=["CB16"])
    p.add("dve", lambda e: e.memset(H, 0.0), writes=["H"])
    p.add("dve", lambda e: e.memset(SM[:, 0:30], 0.0), writes=[("halo", c_) for c_ in range(10)])
    p.add("act", lambda e: e.activation(out=a_bc, in_=cs_("alog"), func=AF.Exp), reads=["CST"], writes=["a_bc"])
    p.add("dve", lambda e: e.tensor_scalar(out=a_bc, in0=a_bc, scalar1=-1.0, scalar2=None, op0=ALU.mult),
          reads=["a_bc"], writes=["a_bc"])
    lv = cs_("lamv")
    p.add("dve", lambda e: e.tensor_tensor(out=yt0[:, 0:64], in0=lv[:, 0:64], in1=lv[:, 64:128], op=ALU.mult),
          reads=["CST"], writes=["yt0"])
    p.add("dve", lambda e: e.tensor_tensor(out=yt0[:, 64:128], in0=lv[:, 128:192], in1=lv[:, 192:256], op=ALU.mult),
          reads=["CST"], writes=["yt0"])
    p.add("dve", lambda e: e.tensor_reduce(out=lam4[:, 0:2], in_=yt0[:, 0:128].rearrange("p (a b) -> p a b", a=2),
                                           axis=AX.X, op=ALU.add), reads=["yt0"], writes=["lam4"])
    p.add("act", lambda e: e.activation(out=lam4[:, 2:4], in_=lam4[:, 0:2], func=AF.Exp), reads=["lam4"], writes=["lam4b"])
    p.add("dve", lambda e: e.tensor_tensor(out=nlam, in0=lam4[:, 3:4], in1=lam4[:, 2:3], op=ALU.subtract),
          reads=["lam4b"], writes=["nlam"])
    p.add("dve", lambda e: e.tensor_scalar(out=nlam, in0=nlam, scalar1=-LAMBDA_INIT, scalar2=None, op0=ALU.add),
          reads=["nlam"], writes=["nlam"])
    p.add("dve", lambda e: e.tensor_scalar(out=gsc, in0=cs_("subln"), scalar1=1.0 - LAMBDA_INIT, scalar2=None, op0=ALU.mult),
          reads=["CST"], writes=["gsc"])

    def emit_rstd(dst, src, n, rkey, wkey, tmpkey):
        p.add("dve", lambda e: e.tensor_scalar(out=dst, in0=src, scalar1=1.0 / n, scalar2=EPS, op0=ALU.mult, op1=ALU.add),
              reads=[rkey], writes=[tmpkey])
        p.add("act", lambda e: e.activation(out=dst, in_=dst, func=AF.Ln), reads=[tmpkey], writes=[tmpkey + "l"])
        p.add("act", lambda e: e.activation(out=dst, in_=dst, func=AF.Exp, scale=-0.5), reads=[tmpkey + "l"], writes=[wkey])

    def emit_xT(row0, dst, dst_key):
        s = st["xslot"]
        st["xslot"] = (s + 1) % 2
        xk = "XS%d" % s
        xt = XS[:, s, :]
        p.add("sp", lambda e: e.dma_start(out=xt, in_=x_d[row0:row0 + 128, :]), writes=[xk], dma="x%d" % s)
        n = st["xnslot"]
        st["xnslot"] ^= 1
        xnk = "xn%d" % n
        sq, rs = ssq[:, n:n + 1], rstd[:, n:n + 1]
        p.add("act", lambda e: e.activation(out=xn[:, n, :], in_=xt, func=AF.Square, accum_out=sq),
              reads=[xk], writes=[xnk, "ssq%d" % n])
        XL = int(os.environ.get('MK_X', '9'))
        if XL < 1:
            return
        emit_rstd(rs, sq, float(D), "ssq%d" % n, "rstd%d" % n, "rstdt%d" % n)
        if XL < 2:
            return
        gbc = cs_("gainbc")
        p.add("dve", lambda e: e.scalar_tensor_tensor(out=xn[:, n, :], in0=xt, scalar=rs, in1=gbc, op0=ALU.mult, op1=ALU.mult),
              reads=[xk, "rstd%d" % n, "CST"], writes=[xnk])
        if XL < 3:
            return
        ptb = PT[:, 0:512].bitcast(BF16).rearrange("p (k t) -> p k t", k=8)
        for kc in range(KC):
            p.add("pe", lambda e, kc=kc: e.transpose(out=ptb[:, kc, :], in_=xn[:, n, kc * 128:(kc + 1) * 128], identity=ident_bf),
                  reads=[xnk, "CB16"], writes=["PT"])
        if XL < 4:
            return
        gT = cs_("gainT")
        XV = int(os.environ.get('MK_XV', '1'))
        if XV == 1:
            p.add("dve", lambda e: e.tensor_copy(out=dst, in_=ptb), reads=["PT", "CST"], writes=[dst_key])
        elif XV == 2:
            for kc in range(KC):
                p.add("dve", lambda e, kc=kc: e.tensor_scalar(out=dst[:, kc, :], in0=ptb[:, kc, :], scalar1=gT[:, kc:kc + 1], scalar2=None, op0=ALU.mult),
                      reads=["PT", "CST"], writes=[dst_key])
        else:
            p.add("dve", lambda e: e.tensor_tensor(out=dst, in0=ptb, in1=gT.unsqueeze(2).to_broadcast([128, 8, 128]), op=ALU.mult),
                  reads=["PT", "CST"], writes=[dst_key])

    def emit_conv(src_raw, n_out, chunk, dst, dst_key, raw_key):
        a = st["acc"]
        st["acc"] ^= 1
        ak = "acc%d" % a
        ac = acc[:, a, 0:n_out]
        cw = cs_("convw")
        cb = cs_("convb")
        w = [cw[:, chunk * 4 + k:chunk * 4 + k + 1] for k in range(4)]
        p.add("dve", lambda e: e.tensor_scalar(out=ac, in0=src_raw[:, 0:n_out], scalar1=w[0], scalar2=None, op0=ALU.mult),
              reads=[raw_key, "CST"], writes=[ak])
        for k in (1, 2, 3):
            p.add("dve", lambda e, k=k: e.scalar_tensor_tensor(out=ac, in0=src_raw[:, k:k + n_out], scalar=w[k], in1=ac,
                                                               op0=ALU.mult, op1=ALU.add),
                  reads=[raw_key, ak, "CST"], writes=[ak])
        p.add("act", lambda e: e.activation(out=dst, in_=ac, func=AF.Silu, bias=cb[:, chunk:chunk + 1]),
              reads=[ak, "CST"], writes=[dst_key])

    own_row = [0]

    def do_x(blk):
        us = blk % 2
        uk = "uT%d" % us
        u = uT[us]
        for t in range(4):
            emit_xT(blk * 512 + t * 128, u[:, :, t * 128:(t + 1) * 128], uk)

    def do_main(blk):
        us = blk % 2
        uk = "uT%d" % us
        u = uT[us]
        for h in range(8):
            pa, pk = next_pab()
            for kc in range(KC):
                p.add("pe", lambda e, h=h, kc=kc, pa=pa: e.matmul(pa[:, :], Wk[:, kc, h * 128:(h + 1) * 128], u[:, kc, :],
                                                              start=(kc == 0), stop=(kc == KC - 1)),
                      reads=[uk, "WA0"], writes=[pk])
            p.add("act", lambda e, h=h, pa=pa: e.activation(out=kst[:, h, :], in_=pa[:, :], func=AF.Copy),
                  reads=[pk], writes=["kst"])
        p.add("sp", lambda e, blk=blk: e.dma_start(out=kT_d[:, :, blk * 512:(blk + 1) * 512].rearrange("h p t -> p h t"), in_=kst),
              reads=["kst"], writes=[("kTd", blk)], dma="kst")

        for t in range(4):
            for hf in range(2):
                pa, pk = next_pab()
                for kc in range(KC):
                    p.add("pe", lambda e, t=t, hf=hf, kc=kc, pa=pa: e.matmul(pa[:, :], u[:, kc, t * 128:(t + 1) * 128],
                                                                            Wv[:, kc, hf * 512:(hf + 1) * 512],
                                                                            start=(kc == 0), stop=(kc == KC - 1)),
                          reads=[uk, "WA1"], writes=[pk])
                p.add("dve", lambda e, t=t, hf=hf, pa=pa: e.tensor_copy(out=vst[:, hf * 4:(hf + 1) * 4, t, :],
                                                                       in_=pa[:, :].rearrange("p (h d) -> p h d", h=4)),
                      reads=[pk], writes=["vst"])
        p.add("sp", lambda e, blk=blk: e.dma_start(
            out=v_d[:, :, blk * 512:(blk + 1) * 512].rearrange("h p (t d) -> p h t d", t=4), in_=vst),
            reads=["vst"], writes=[("vd", blk)], dma="vst")

        for ch in range(10):
            pa, pk = next_pab()
            wsrc = Wx[:, :, ch * 128:(ch + 1) * 128] if ch < 8 else WC[:, :, (ch - 8) * 128:(ch - 7) * 128]
            wkey = "WB0" if ch < 8 else "WC"
            for kc in range(KC):
                p.add("pe", lambda e, kc=kc, pa=pa, wsrc=wsrc: e.matmul(pa[:, :], wsrc[:, kc, :], u[:, kc, :],
                                                                    start=(kc == 0), stop=(kc == KC - 1)),
                      reads=[uk, wkey], writes=[pk])
            rs_ = ch % 2
            rk = "raw%d" % rs_
            p.add("pool", lambda e, ch=ch, rs_=rs_: e.tensor_copy(out=raw[:, rs_, 0:3], in_=halo[:, ch, :]),
                  reads=[("halo", ch)], writes=[rk])
            p.add("act", lambda e, pa=pa, rs_=rs_: e.activation(out=raw[:, rs_, 3:515], in_=pa[:, :], func=AF.Copy),
                  reads=[pk], writes=[rk])
            p.add("pool", lambda e, ch=ch, rs_=rs_: e.tensor_copy(out=halo[:, ch, :], in_=raw[:, rs_, 512:515]),
                  reads=[rk], writes=[("halo", ch)])
            emit_conv(raw[:, rs_, :], 512, ch, xsT[:, ch, :], "xsT", rk)

    def do_rest(blk):
        us = blk % 2
        uk = "uT%d" % us
        u = uT[us]
        ptb = PT[:, 0:512].bitcast(BF16).rearrange("p (k t) -> p k t", k=8)
        for t in range(4):
            for ch in range(8):
                p.add("pe", lambda e, t=t, ch=ch: e.transpose(out=ptb[:, ch, :], in_=xsT[:, ch, t * 128:(t + 1) * 128], identity=ident_bf),
                      reads=["xsT", "CB16"], writes=["PT"])
            p.add("dve", lambda e, t=t: e.tensor_copy(out=xs_tm[:, t, :].rearrange("p (k c) -> p k c", k=8), in_=ptb),
                  reads=["PT"], writes=[("xs_tm", t)])
            for g in range(2):
                p.add("pe", lambda e, t=t, g=g: e.transpose(out=ptb[:, g, :], in_=xsT[:, 8 + g, t * 128:(t + 1) * 128], identity=ident_bf),
                      reads=["xsT", "CB16"], writes=["PT"])
            p.add("dve", lambda e, t=t: e.tensor_copy(out=B_tm[:, t, :].rearrange("p (k c) -> p k c", k=2), in_=ptb[:, 0:2, :]),
                  reads=["PT"], writes=[("B_tm", t)])

        pmd = PM[:, 256:320].rearrange("p (t h) -> p t h", t=4)
        for t in range(4):
            for kc in range(KC):
                p.add("pe", lambda e, t=t, kc=kc: e.matmul(pmd[:, t, :], u[:, kc, t * 128:(t + 1) * 128], WC[:, kc, 512:528],
                                                           start=(kc == 0), stop=(kc == KC - 1)),
                      reads=[uk, "WC"], writes=["PM"])
        dtb = cs_("dtb")
        p.add("dve", lambda e: e.tensor_tensor(out=dtz, in0=pmd, in1=dtb.unsqueeze(1).to_broadcast([128, 4, 16]), op=ALU.add),
              reads=["PM", "CST"], writes=["dtz"])
        p.add("act", lambda e: e.activation(out=dta, in_=dtz, func=AF.Abs), reads=["dtz"], writes=["dta"])
        p.add("act", lambda e: e.activation(out=dte, in_=dta, func=AF.Exp, scale=-1.0), reads=["dta"], writes=["dte"])
        p.add("act", lambda e: e.activation(out=dte, in_=dte, func=AF.Ln, bias=1.0), reads=["dte"], writes=["dtl"])
        p.add("dve", lambda e: e.scalar_tensor_tensor(out=dtv, in0=dtz, scalar=0.0, in1=dte, op0=ALU.max, op1=ALU.add),
              reads=["dtz", "dtl"], writes=["dtv"])
        vl = cs_("valid")
        p.add("dve", lambda e, blk=blk: e.tensor_tensor(out=dtv, in0=dtv,
                                                        in1=vl[:, blk * 4:(blk + 1) * 4].unsqueeze(2).to_broadcast([128, 4, 16]), op=ALU.mult),
              reads=["dtv", "CST"], writes=["dtv"])
        p.add("dve", lambda e: e.tensor_tensor(out=adt, in0=dtv, in1=a_bc.unsqueeze(1).to_broadcast([128, 4, 16]), op=ALU.mult),
              reads=["dtv", "a_bc"], writes=["adt"])

        own_blk = (blk % 2 == 1)
        if own_blk:
            for g in range(2):
                pa, pk = next_pab()
                for kc in range(KC):
                    p.add("pe", lambda e, g=g, kc=kc, pa=pa: e.matmul(pa[:, 0:260], WC[:, kc, 256 + g * 128:256 + (g + 1) * 128],
                                                                  u[:, kc, 252:512], start=(kc == 0), stop=(kc == KC - 1)),
                          reads=[uk, "WC"], writes=[pk])
                p.add("act", lambda e, g=g, pa=pa: e.activation(out=Craw[:, g, :], in_=pa[:, 0:260], func=AF.Copy),
                      reads=[pk], writes=["yt1"])
                emit_conv(Craw[:, g, 1:260], 256, 10 + g, Cact[:, g, :], "Cact", "yt1")

        tri = cs_("tri")
        pmc = PM[:, 320:384].rearrange("p (t h) -> p t h", t=4)
        pmt = PM[:, 384:448].rearrange("p (t h) -> p t h", t=4)
        for t in range(4):
            p.add("pe", lambda e, t=t: e.matmul(pmc[:, t, :], tri, adt[:, t, :], start=True, stop=True),
                  reads=["adt", "CST"], writes=["PM"])
        for t in range(4):
            p.add("pe", lambda e, t=t: e.matmul(pmt[:, t, :], cs_("ones"), adt[:, t, :], start=True, stop=True),
                  reads=["adt", "CST"], writes=["PM"])
        p.add("dve", lambda e: e.tensor_copy(out=csb4, in_=pmc), reads=["PM"], writes=["csb"])
        p.add("dve", lambda e: e.tensor_copy(out=tot4, in_=pmt), reads=["PM"], writes=["tot"])
        p.add("dve", lambda e: e.tensor_tensor(out=dec4, in0=tot4, in1=csb4, op=ALU.subtract), reads=["tot", "csb"], writes=["dec"])
        p.add("act", lambda e: e.activation(out=dec4, in_=dec4, func=AF.Exp), reads=["dec"], writes=["dece"])
        p.add("act", lambda e: e.activation(out=cd4, in_=tot4, func=AF.Exp), reads=["tot"], writes=["cd"])
        p.add("dve", lambda e: e.tensor_tensor(out=wsc4, in0=dec4, in1=dtv, op=ALU.mult), reads=["dece", "dtv"], writes=["wsc"])

        for t in range(4):
            own = own_blk and t >= 2
            csb = csb4[:, t, :]
            cd = cd4[:, t, :]
            wsc = wsc4[:, t, :]
            xs_ = t % 2
            xk_ = "xrd%d" % xs_
            p.add("pool", lambda e, t=t, xs_=xs_, wsc=wsc: e.tensor_tensor(out=xrd[:, xs_, :].rearrange("p (h d) -> p h d", h=16),
                                                                 in0=xs_tm[:, t, :].rearrange("p (h d) -> p h d", h=16),
                                                                 in1=wsc.unsqueeze(2).to_broadcast([128, 16, 64]), op=ALU.mult),
                  reads=[("xs_tm", t), "wsc"], writes=[xk_])
            for g in range(2):
                psg = PS0 if g == 0 else PS1
                p.add("pe", lambda e, t=t, g=g, psg=psg, xs_=xs_: e.matmul(psg[:, :], B_tm[:, t, g * 128:(g + 1) * 128],
                                                                         xrd[:, xs_, g * 512:(g + 1) * 512], start=True, stop=True),
                      reads=[("B_tm", t), xk_], writes=["PS%d" % g])
            if own:
                p.add("pool", lambda e: e.tensor_copy(out=Hb, in_=H), reads=["H"], writes=["Hb"])
            p.add("dve", lambda e, cd=cd: e.tensor_tensor(out=H.rearrange("p (h d) -> p h d", h=16), in0=H.rearrange("p (h d) -> p h d", h=16),
                                                   in1=cd.unsqueeze(2).to_broadcast([128, 16, 64]), op=ALU.mult),
                  reads=["H", "cd"], writes=["H"])
            for g in range(2):
                psg = PS0 if g == 0 else PS1
                p.add("dve", lambda e, g=g, psg=psg: e.tensor_tensor(out=H[:, g * 512:(g + 1) * 512], in0=H[:, g * 512:(g + 1) * 512],
                                                                   in1=psg[:, :], op=ALU.add),
                      reads=["H", "PS%d" % g], writes=["H"])
            if not own:
                continue

            tt = t - 2
            for hf in range(2):
                pa, pk = next_pab()
                for kc in range(KC):
                    p.add("pe", lambda e, t=t, hf=hf, kc=kc, pa=pa: e.matmul(pa[:, :], u[:, kc, t * 128:(t + 1) * 128],
                                                                            Wz[:, kc, hf * 512:(hf + 1) * 512],
                                                                            start=(kc == 0), stop=(kc == KC - 1)),
                          reads=[uk, "WB1"], writes=[pk])
                p.add("act", lambda e, hf=hf, pa=pa: e.activation(out=zs[:, hf * 512:(hf + 1) * 512], in_=pa[:, :], func=AF.Silu),
                      reads=[pk], writes=["zs"])
            CBm = yt1[:, 768:1024].rearrange("p (g t) -> p g t", g=2)
            for g in range(2):
                p.add("pe", lambda e, t=t, g=g, tt=tt: e.matmul(PM[:, g * 128:(g + 1) * 128], xsT[:, 8 + g, t * 128:(t + 1) * 128],
                                                              Cact[:, g, tt * 128:(tt + 1) * 128], start=True, stop=True),
                      reads=["xsT", "Cact"], writes=["PM"])
                p.add("dve", lambda e, g=g: e.tensor_tensor(out=CBm[:, g, :], in0=PM[:, g * 128:(g + 1) * 128], in1=tri, op=ALU.mult),
                      reads=["PM", "CST"], writes=["CBm"])
            p.add("pool", lambda e, t=t: e.tensor_tensor(out=xr.rearrange("p (h d) -> p h d", h=16),
                                                        in0=xs_tm[:, t, :].rearrange("p (h d) -> p h d", h=16),
                                                        in1=dtv[:, t, :].unsqueeze(2).to_broadcast([128, 16, 64]), op=ALU.mult),
                  reads=[("xs_tm", t), "dtv"], writes=["xr"])
            ustr = cs_("ustr")
            for hq in range(4):
                g = hq // 2
                pa, pk = next_pab()
                for hh in range(4):
                    h = hq * 4 + hh
                    p.add("dve", lambda e, t=t, h=h, hh=hh: e.tensor_scalar(out=A_[:, hh, :], in0=ustr, scalar1=adt[:, t, h:h + 1],
                                                                            scalar2=None, op0=ALU.mult),
                          reads=["adt", "CST"], writes=[("A", hh)])
                    p.add("pe", lambda e, hh=hh, pa=pa: e.matmul(pa[:, hh * 128:(hh + 1) * 128], A_[:, hh, :], tri, start=True, stop=True),
                          reads=[("A", hh), "CST"], writes=[pk])
                p.add("act", lambda e, pa=pa: e.activation(out=Lt.rearrange("p h t -> p (h t)"), in_=pa[:, :], func=AF.Exp),
                      reads=[pk], writes=["Lt"])
                p.add("dve", lambda e, g=g: e.tensor_tensor(out=MT, in0=Lt, in1=CBm[:, g, :].unsqueeze(1).to_broadcast([128, 4, 128]), op=ALU.mult),
                      reads=["Lt", "CBm"], writes=["MT"])
                for hh in range(4):
                    h = hq * 4 + hh
                    py = PY0 if h < 8 else PY1
                    c0 = (h % 8) * 64
                    p.add("pe", lambda e, hh=hh, h=h, py=py, c0=c0: e.matmul(py[:, c0:c0 + 64], MT[:, hh, :], xr[:, h * 64:(h + 1) * 64],
                                                                          start=True, stop=True),
                          reads=["MT", "xr"], writes=["PY%d" % (h // 8)])
            for g in range(2):
                psg = PS0 if g == 0 else PS1
                p.add("pe", lambda e, g=g, psg=psg, tt=tt: e.matmul(psg[:, :], Cact[:, g, tt * 128:(tt + 1) * 128], Hb[:, g * 512:(g + 1) * 512],
                                                                  start=True, stop=True),
                      reads=["Cact", "Hb"], writes=["PS%d" % g])
            p.add("act", lambda e, csb=csb: e.activation(out=ecs, in_=csb, func=AF.Exp), reads=["csb"], writes=["ecs"])
            dsk = cs_("dskip")
            for g in range(2):
                psg = PS0 if g == 0 else PS1
                py = PY0 if g == 0 else PY1
                ysl = yt0[:, g * 512:(g + 1) * 512]
                p.add("act", lambda e, psg=psg, ysl=ysl: e.activation(out=ysl, in_=psg[:, :], func=AF.Copy),
                      reads=["PS%d" % g], writes=["yt0"])
                p.add("dve", lambda e, g=g, ysl=ysl: e.tensor_tensor(
                    out=ysl.rearrange("p (h d) -> p h d", h=8), in0=ysl.rearrange("p (h d) -> p h d", h=8),
                    in1=ecs[:, g * 8:(g + 1) * 8].unsqueeze(2).to_broadcast([128, 8, 64]), op=ALU.mult),
                    reads=["yt0", "ecs"], writes=["yt0"])
                p.add("dve", lambda e, py=py, ysl=ysl: e.tensor_tensor(out=ysl, in0=ysl, in1=py[:, :], op=ALU.add),
                      reads=["yt0", "PY%d" % g], writes=["yt0"])
            ysk = yt1[:, 0:512]
            for g in range(2):
                ysl = yt0[:, g * 512:(g + 1) * 512]
                p.add("dve", lambda e, t=t, g=g: e.tensor_tensor(
                    out=ysk.rearrange("p (h d) -> p h d", h=8), in0=xs_tm[:, t, g * 512:(g + 1) * 512].rearrange("p (h d) -> p h d", h=8),
                    in1=dsk[:, g * 8:(g + 1) * 8].unsqueeze(2).to_broadcast([128, 8, 64]), op=ALU.mult),
                    reads=[("xs_tm", t), "CST"], writes=["yt1"])
                p.add("dve", lambda e, ysl=ysl: e.tensor_tensor(out=ysl, in0=ysl, in1=ysk, op=ALU.add), reads=["yt0", "yt1"], writes=["yt0"])
                p.add("dve", lambda e, g=g, ysl=ysl: e.tensor_tensor(out=ysl, in0=ysl, in1=zs[:, g * 512:(g + 1) * 512], op=ALU.mult),
                      reads=["yt0", "zs"], writes=["yt0"])
                p.add("act", lambda e, g=g, ysl=ysl: e.activation(out=ysk, in_=ysl, func=AF.Square, accum_out=gss[:, g:g + 1]),
                      reads=["yt0"], writes=["yt1", "gss"])
            emit_rstd(grs, gss, 512.0, "gss", "grs", "grst")
            sg = cs_("ssdg")
            for g in range(2):
                ysl = yt0[:, g * 512:(g + 1) * 512]
                p.add("dve", lambda e, g=g, ysl=ysl: e.scalar_tensor_tensor(out=mixed[:, g * 512:(g + 1) * 512], in0=ysl, scalar=grs[:, g:g + 1],
                                                                          in1=sg[:, g * 512:(g + 1) * 512], op0=ALU.mult, op1=ALU.mult),
                      reads=["yt0", "grs", "CST"], writes=["mixed"])
            r0 = own_row[0]
            own_row[0] += 128
            p.add("sp", lambda e, r0=r0: e.dma_start(out=mt_d[r0:r0 + 128, :], in_=mixed), reads=["mixed"], writes=[("mtd", r0)], dma="mix")

    do_x(0)
    for blk in range(NB):
        do_main(blk)
        if blk + 1 < NB:
            do_x(blk + 1)
        do_rest(blk)
    p.barrier()
    if STAGE < 9:
        p.emit(nc, es)
        es.close()
        return nc

    load_w(Wq, C_Q, 1024, "WA0", "wA0")
    load_w(Wza, C_ZA, 1024, "WA1", "wA1")
    for ot in range(NOT):
        i, tt = divmod(ot, 2)
        emit_xT(i * 1024 + 768 + tt * 128, uTown[:, :, ot * 128:(ot + 1) * 128], "uTown")

    kpos = cs_("kpos")
    qref = cs_("qref")
    QT = FS[:, 5126:5126 + OWN].bitcast(BF16).rearrange("p (m t) -> p m t", m=2)
    p.add("pool", lambda e: e.memset(QT, 0.0), writes=["QT"])
    zsT = FS[:, 0:OWN]
    rL = FS[:, 2054:2566]
    On = FS[:, 2566:3078]
    Aa = FS[:, 3078:3334]
    Asq = FS[:, 3334:3590]
    rs2 = FS[:, 3590:3846]
    biasT = [FS[:, 4102 + s_ * 512:4102 + s_ * 512 + NSB * NT].rearrange("p (i n) -> p i n", i=NSB) for s_ in range(2)]
    st["sslot"] = 0
    st["ya"] = 0
    ones32 = cs_("ones")

    def emit_kv_load(h):
        ks = h % 2
        p.add("sp", lambda e: e.dma_start(out=G[ks][:, 0:S], in_=kT_d[h, :, :]), writes=["kT%d" % ks], dma="kld%d" % ks)
        p.add("sp", lambda e: e.dma_start(out=G[2 + ks][:, 0:NT * 128], in_=v_d[h, :, :]),
              writes=["V%d" % ks] + (["xn0", "xn1"] if ks == 0 else []), dma="vld%d" % ks)

    pending = [None]

    def flush_epi():
        if pending[0] is not None:
            f = pending[0]
            pending[0] = None
            f()

    emit_kv_load(0)
    for h in range(8):
        ks = h % 2
        kTh = G[ks][:, 0:S]
        Vh = G[2 + ks][:, 0:NT * 128].rearrange("p (t d) -> p t d", d=128)
        kk, vk, bk = "kT%d" % ks, "V%d" % ks, "biasT%d" % ks
        flush_epi()
        for i in range(NSB):
            p.add("dve", lambda e, i=i, h=h, ks=ks: e.tensor_scalar(out=biasT[ks][:, i, :], in0=kpos, scalar1=qref[:, i:i + 1],
                                                                   scalar2=SLOPES[h], op0=ALU.subtract, op1=ALU.mult),
                  reads=["CST"], writes=[bk])
        for c in range(0, OWN, 512):
            n_ = min(512, OWN - c)
            pa, pk = next_pab()
            for kc in range(KC):
                p.add("pe", lambda e, h=h, kc=kc, pa=pa, c=c, n_=n_: e.matmul(pa[:, 0:n_], Wq[:, kc, h * 128:(h + 1) * 128], uTown[:, kc, c:c + n_],
                                                                            start=(kc == 0), stop=(kc == KC - 1)),
                      reads=["uTown", "WA0"], writes=[pk])
            p.add("act", lambda e, pa=pa, c=c, n_=n_: e.activation(out=QT[0:64, 0, c:c + n_], in_=pa[0:64, 0:n_], func=AF.Copy), reads=[pk], writes=["QT"])
            p.add("act", lambda e, pa=pa, c=c, n_=n_: e.activation(out=QT[64:128, 1, c:c + n_], in_=pa[64:128, 0:n_], func=AF.Copy), reads=[pk], writes=["QT"])
            pa, pk = next_pab()
            for kc in range(KC):
                p.add("pe", lambda e, h=h, kc=kc, pa=pa, c=c, n_=n_: e.matmul(pa[:, 0:n_], Wza[:, kc, h * 128:(h + 1) * 128], uTown[:, kc, c:c + n_],
                                                                            start=(kc == 0), stop=(kc == KC - 1)),
                      reads=["uTown", "WA1"], writes=[pk])
            p.add("act", lambda e, pa=pa, c=c, n_=n_: e.activation(out=zsT[:, c:c + n_], in_=pa[:, 0:n_], func=AF.Silu), reads=[pk], writes=["zsT"])
        if h + 1 < 8:
            emit_kv_load(h + 1)

        for i in range(NSB):
            nk = 8 * i + 8
            q0 = i * 256
            if i % 2 == 1:
                PO_, PL_, ok_, lk_ = PY0, PY1, "PY0", "PY1"
            else:
                PO_, PL_, ok_, lk_ = PA, PBk, "PA", "PB"

            def emit_qk(n, i=i, nk=nk, q0=q0, kTh=kTh):
                ss = st["sslot"]
                st["sslot"] ^= 1
                Sb = PS0 if ss == 0 else PS1
                sk = "S%d" % ss
                diag = n >= nk - 2
                for m in range(2):
                    p.add("pe", lambda e, m=m, n=n, Sb=Sb, diag=diag: e.matmul(Sb[:, m * 256:(m + 1) * 256],
                                                                             kTh[:, n * 128:(n + 1) * 128],
                                                                             QT[:, m, q0:q0 + 256], start=True, stop=(not diag)),
                          reads=[kk, "QT"], writes=[sk])
                    if diag:
                        mk = maskA_bf if n == nk - 2 else maskB_bf
                        p.add("pe", lambda e, m=m, Sb=Sb, mk=mk: e.matmul(Sb[:, m * 256:(m + 1) * 256], ident_bf, mk, start=False, stop=True),
                              reads=["CB16"], writes=[sk])
                return Sb, sk

            pend = emit_qk(0)
            for n in range(nk):
                Sb, sk = pend
                if n + 1 < nk:
                    pend = emit_qk(n + 1)
                pslot = st["pts"]
                st["pts"] = (pslot + 1) % 3
                pk_ = "PTS%d" % pslot
                p.add("act", lambda e, Sb=Sb, pslot=pslot, i=i, n=n, ks=ks: e.activation(out=PTS[:, pslot, :], in_=Sb[:, :], func=AF.Exp,
                                                                                      bias=biasT[ks][:, i, n:n + 1], scale=0.125),
                      reads=[sk, bk], writes=[pk_])
                p.add("pe", lambda e, n=n, pslot=pslot, nk=nk, Vh=Vh, PO_=PO_: e.matmul(PO_[:, :], Vh[:, n, :], PTS[:, pslot, :], start=(n == 0), stop=(n == nk - 1)),
                      reads=[vk, pk_], writes=[ok_])
                p.add("pe", lambda e, n=n, pslot=pslot, nk=nk, PL_=PL_: e.matmul(PL_[:, :], ones_bf, PTS[:, pslot, :], start=(n == 0), stop=(n == nk - 1)),
                      reads=["CB16", pk_], writes=[lk_])
                if n == min(5, nk - 1):
                    flush_epi()

            flush_epi()
            p.add("dve", lambda e, PL_=PL_: e.reciprocal(out=rL, in_=PL_[:, :]), reads=[lk_], writes=["rL"])
            p.add("dve", lambda e, PO_=PO_: e.tensor_tensor(out=On, in0=PO_[:, :], in1=rL, op=ALU.mult), reads=[ok_, "rL"], writes=["On"])
            p.add("dve", lambda e: e.scalar_tensor_tensor(out=Aa, in0=On[:, 256:512], scalar=nlam[:, 0:1], in1=On[:, 0:256], op0=ALU.mult, op1=ALU.add),
                  reads=["On", "nlam"], writes=["Aa"])
            p.add("dve", lambda e: e.tensor_tensor(out=Asq, in0=Aa, in1=Aa, op=ALU.mult), reads=["Aa"], writes=["Asq"])

            def epi2(h=h, i=i, q0=q0):
                p.add("pe", lambda e: e.matmul(PM[:, 0:256], ones32, Asq, start=True, stop=True), reads=["Asq", "CST"], writes=["PM"])
                emit_rstd(rs2, PM[:, 0:256], 128.0, "PM", "rs2", "rs2t")
                p.add("dve", lambda e: e.tensor_tensor(out=Aa, in0=Aa, in1=rs2, op=ALU.mult), reads=["Aa", "rs2"], writes=["Aa"])
                ys_ = st["ya"]
                st["ya"] ^= 1
                yab = BM[:, ys_ * 256:(ys_ + 1) * 256]
                p.add("dve", lambda e: e.scalar_tensor_tensor(out=yab, in0=Aa, scalar=gsc[:, 0:1], in1=zsT[:, q0:q0 + 256], op0=ALU.mult, op1=ALU.mult),
                      reads=["Aa", "gsc", "zsT"], writes=["ya%d" % ys_])
                p.add("sp", lambda e: e.dma_start(out=ma_d[h, :, q0:q0 + 256], in_=yab), reads=["ya%d" % ys_],
                      writes=[("mad", h, i)], dma="ya%d" % ys_)

            pending[0] = epi2
    flush_epi()

    p.barrier()
    if STAGE < 10:
        p.emit(nc, es)
        es.close()
        return nc
    p.add("pool", lambda e: e.dma_start(out=Wo, in_=wout_d.rearrange("(k p) c -> p k c", p=128)), writes=["Wo"], dma="wB0")
    hb = FS[:, 0:1024]
    fss = SM[:, 40:41]
    frs = SM[:, 41:42]
    fg = cs_("fing")
    outs = []
    ptb = PT[:, 0:512].bitcast(BF16).rearrange("p (k t) -> p k t", k=8)
    for ot in range(NOT):
        i, tt = divmod(ot, 2)
        row0 = i * 1024 + 768 + tt * 128
        ms = ot % 2
        mx = BM[:, ms * 1024:(ms + 1) * 1024]
        mTa = G[1][:, ms * 1024:(ms + 1) * 1024].rearrange("p (h t) -> p h t", h=8)
        mTs = G[0][:, ms * 1024:(ms + 1) * 1024].rearrange("p (h t) -> p h t", h=8)
        ob = FS[:, 1024 + ms * 1024:2048 + ms * 1024]
        p.add("sp", lambda e, mx=mx, ot=ot: e.dma_start(out=mx, in_=mt_d[ot * 128:(ot + 1) * 128, :]), writes=["mx%d" % ms], dma="mx%d" % ms)
        p.add("sp", lambda e, mTa=mTa, ot=ot: e.dma_start(out=mTa, in_=ma_d[:, :, ot * 128:(ot + 1) * 128].rearrange("h p t -> p h t")),
              writes=["mTa%d" % ms], dma="mta%d" % ms)
        s_ = st["xslot"]
        st["xslot"] = (s_ + 1) % 2
        xt = XS[:, s_, :]
        p.add("sp", lambda e, xt=xt, row0=row0: e.dma_start(out=xt, in_=x_d[row0:row0 + 128, :]), writes=["XS%d" % s_], dma="x%d" % s_)
        for kc in range(8):
            p.add("pe", lambda e, kc=kc, mx=mx: e.transpose(out=ptb[:, kc, :], in_=mx[:, kc * 128:(kc + 1) * 128], identity=ident_bf),
                  reads=["mx%d" % ms, "CB16"], writes=["PT"])
        p.add("dve", lambda e, mTs=mTs: e.tensor_copy(out=mTs, in_=ptb), reads=["PT"], writes=["mTs%d" % ms])
        for hf in range(2):
            pa, pk = next_pab()
            for kc in range(16):
                lh = mTs[:, kc, :] if kc < 8 else mTa[:, kc - 8, :]
                lk = ("mTs%d" % ms) if kc < 8 else ("mTa%d" % ms)
                p.add("pe", lambda e, kc=kc, hf=hf, pa=pa, lh=lh: e.matmul(pa[:, :], lh, Wo[:, kc, hf * 512:(hf + 1) * 512],
                                                                       start=(kc == 0), stop=(kc == 15)),
                      reads=[lk, "Wo"], writes=[pk])
            p.add("dve", lambda e, hf=hf, pa=pa, xt=xt: e.tensor_tensor(out=hb[:, hf * 512:(hf + 1) * 512], in0=pa[:, :],
                                                                      in1=xt[:, hf * 512:(hf + 1) * 512], op=ALU.add),
                  reads=[pk, "XS%d" % s_], writes=["hb"])
        p.add("act", lambda e, ob=ob: e.activation(out=ob, in_=hb, func=AF.Square, accum_out=fss), reads=["hb"], writes=["ob%d" % ms, "fss"])
        emit_rstd(frs, fss, float(D), "fss", "frs", "frst")
        p.add("dve", lambda e, ob=ob: e.scalar_tensor_tensor(out=ob, in0=hb, scalar=frs[:, 0:1], in1=fg, op0=ALU.mult, op1=ALU.mult),
              reads=["hb", "frs", "CST"], writes=["ob%d" % ms])
        outs.append(p.add("sp", lambda e, ob=ob, ot=ot: e.dma_start(out=out_d[ot * 128:(ot + 1) * 128, :], in_=ob), reads=["ob%d" % ms],
                          writes=[("outd", ot)], dma="ob%d" % ms))
    p.wait_nodes("sp", outs)
    p.emit(nc, es)
    es.close()
    return nc


_CACHE = {}


def _cst_table(S, j, params):
    NT = S // 128
    NSB = S // 1024
    off, ncst = cst_layout(S)
    t = np.zeros((128, ncst), np.float32)

    def put(name, arr):
        o, n = off[name]
        t[:, o:o + n] = arr

    pidx = np.arange(128)
    put("gainT", params["norm_gain"].reshape(8, 128).T)
    cw = params["conv_w"].reshape(4, 12, 128)
    put("convw", cw.transpose(2, 1, 0).reshape(128, 48))
    put("convb", params["conv_b"].reshape(12, 128).T)
    put("dtb", np.broadcast_to(params["dt_bias"].reshape(1, 16), (128, 16)))
    put("alog", np.broadcast_to(params["a_log"].reshape(1, 16), (128, 16)))
    put("dskip", np.broadcast_to(params["d_skip"].reshape(1, 16), (128, 16)))
    lam = np.concatenate([params["lambda_q1"].reshape(-1), params["lambda_k1"].reshape(-1),
                          params["lambda_q2"].reshape(-1), params["lambda_k2"].reshape(-1)])
    put("lamv", np.broadcast_to(lam.reshape(1, 256), (128, 256)))
    put("subln", params["subln_gain"].reshape(128, 1))
    put("ident", np.eye(128, dtype=np.float32))
    tri = (pidx[:, None] <= pidx[None, :]).astype(np.float32)
    put("tri", tri)
    put("ustr", 1.0 - tri)
    q = np.arange(256)
    put("maskA", np.where(pidx[:, None] <= q[None, :], 0.0, NEG).astype(np.float32))
    put("maskB", np.where(128 + pidx[:, None] <= q[None, :], 0.0, NEG).astype(np.float32))
    pad = (3 - j) * 256
    gpos = (np.arange(NT)[None, :] * 128 + pidx[:, None] - pad).astype(np.float32)
    put("kpos", np.where(gpos >= 0, gpos, -1.0e9).astype(np.float32))
    put("valid", (gpos >= 0).astype(np.float32))
    qr = (np.arange(NSB) * 1024 + 256 * j + 128).astype(np.float32)
    put("qref", np.broadcast_to(qr.reshape(1, NSB), (128, NSB)))
    put("ssdg", np.broadcast_to(params["ssd_norm_gain"].reshape(1, 1024), (128, 1024)))
    put("fing", np.broadcast_to(params["final_norm_gain"].reshape(1, 1024), (128, 1024)))
    put("ones", np.ones((128, 128), np.float32))
    put("gainbc", np.broadcast_to(params["norm_gain"].reshape(1, 1024), (128, 1024)))
    return t


def kernel(x, norm_gain, w_in, conv_w, conv_b, dt_bias, a_log, d_skip, ssd_norm_gain,
           lambda_q1, lambda_k1, lambda_q2, lambda_k2, subln_gain, w_out, final_norm_gain, _trace=False):
    x = np.asarray(x, np.float32)
    B, S, _ = x.shape
    assert B == 2 and S % 1024 == 0
    params = dict(norm_gain=np.asarray(norm_gain, np.float32), conv_w=np.asarray(conv_w, np.float32),
                  conv_b=np.asarray(conv_b, np.float32), dt_bias=np.asarray(dt_bias, np.float32),
                  a_log=np.asarray(a_log, np.float32), d_skip=np.asarray(d_skip, np.float32),
                  ssd_norm_gain=np.asarray(ssd_norm_gain, np.float32), lambda_q1=np.asarray(lambda_q1, np.float32),
                  lambda_k1=np.asarray(lambda_k1, np.float32), lambda_q2=np.asarray(lambda_q2, np.float32),
                  lambda_k2=np.asarray(lambda_k2, np.float32), subln_gain=np.asarray(subln_gain, np.float32),
                  final_norm_gain=np.asarray(final_norm_gain, np.float32))
    w_in2 = np.ascontiguousarray(np.asarray(w_in, np.float32).reshape(D, NCOL))
    w_out2 = np.ascontiguousarray(np.asarray(w_out, np.float32).reshape(2048, D))
    if S not in _CACHE:
        _CACHE[S] = build(S)
    nc = _CACHE[S]
    in_maps = []
    for c in range(8):
        b, j = divmod(c, 4)
        pad = (3 - j) * 256
        xc = np.zeros((S, D), np.float32)
        xc[pad:] = x[b, :S - pad]
        in_maps.append({"x": xc, "w_in": w_in2, "w_out": w_out2, "cst": _cst_table(S, j, params)})
    res = run_bass_kernel_spmd(nc, in_maps, core_ids=list(range(8)), trace=_trace)
    out = np.zeros((B, S, D), np.float32)
    NSB = S // 1024
    for c in range(8):
        b, j = divmod(c, 4)
        o = np.asarray(res.results[c]["out"], np.float32).reshape(NSB, 256, D)
        for i in range(NSB):
            g0 = i * 1024 + 256 * j
            out[b, g0:g0 + 256] = o[i]
    if _trace:
        kernel.last_res = res
    return out
```
